# Optimizing a Trainium2 kernel written in Bass

```python
import jax, jax.numpy as jnp
from jax import lax
import numpy as np

D_MODEL = 1024
BATCH = 2
SEQ = 8192
DEPTH = 1

HEAD_DIM = 128
ATT_HEADS = 4
DILATED_PAIRS = ((128, 1), (512, 4), (2048, 16))
N_DIL_GROUPS = len(DILATED_PAIRS)
Q_W = N_DIL_GROUPS * ATT_HEADS * HEAD_DIM
KV_W = ATT_HEADS * HEAD_DIM
POOL_WINDOWS = (2, 4, 8, 16)
N_POOL_GROUPS = len(POOL_WINDOWS)
POOL_GROUP_DIM = 128
POOL_W = N_POOL_GROUPS * POOL_GROUP_DIM
MIX_W = KV_W + POOL_W
IN_W = Q_W + 2 * KV_W + POOL_W
BLOCK = 128
ROT_DIM = HEAD_DIM // 4
ROT_HALF = ROT_DIM // 2
ROPE_THETA = 500000.0
N_MEM = 256
X_HEADS = 4
X_W = X_HEADS * HEAD_DIM
D_FF = ((8 * D_MODEL // 3 + 255) // 256) * 256
EPS = 1e-6
NEG_INF = -1e30

kernel_name = "hybrid_pool_dilated_attn_block"


def rms_norm(x, g):
    xf = x.astype(jnp.float32)
    y = xf * lax.rsqrt(jnp.mean(xf * xf, axis=-1, keepdims=True) + EPS)
    return (y * g.astype(jnp.float32)).astype(x.dtype)


def rope_partial(t, cos, sin):
    ex = tuple(range(2, t.ndim - 1))
    c = jnp.expand_dims(cos, ex)
    s = jnp.expand_dims(sin, ex)
    tr = t[..., :ROT_DIM].astype(jnp.float32)
    x1, x2 = tr[..., :ROT_HALF], tr[..., ROT_HALF:]
    rot = jnp.concatenate([x1 * c - x2 * s, x2 * c + x1 * s], axis=-1).astype(t.dtype)
    return jnp.concatenate([rot, t[..., ROT_DIM:]], axis=-1)


def dilated_branch(q, k, v, window, dilation):
    B, S, H, D = q.shape
    n_back = window // dilation
    L = S // dilation
    nb = -(-L // BLOCK)
    Lp = nb * BLOCK

    def to_sub(t):
        t = t.astype(jnp.float32).reshape(B, L, dilation, H, D).transpose(0, 2, 3, 1, 4)
        return jnp.pad(t, ((0, 0), (0, 0), (0, 0), (0, Lp - L), (0, 0)))

    def windows(t):
        tp = jnp.pad(t, ((0, 0), (0, 0), (0, 0), (BLOCK, 0), (0, 0)))
        tp = tp.reshape(B, dilation, H, nb + 1, BLOCK, D)
        return jnp.concatenate([tp[:, :, :, :-1], tp[:, :, :, 1:]], axis=4)

    qb = to_sub(q).reshape(B, dilation, H, nb, BLOCK, D)
    kw = windows(to_sub(k))
    vw = windows(to_sub(v))
    s = jnp.einsum('bdhnqc,bdhnkc->bdhnqk', qb, kw)
    qi = jnp.arange(BLOCK)[:, None]
    kj = jnp.arange(2 * BLOCK)[None, :]
    delta = qi + BLOCK - kj
    key_idx = jnp.arange(nb)[:, None, None] * BLOCK - BLOCK + kj[None]
    valid = (delta >= 0) & (delta <= n_back) & (key_idx >= 0)
    s = jnp.where(valid, s, NEG_INF)
    m = jnp.max(s, axis=-1)
    p = jnp.exp(s - m[..., None])
    l = jnp.sum(p, axis=-1)
    acc = jnp.einsum('bdhnqk,bdhnkc->bdhnqc', p, vw)
    acc = acc.reshape(B, dilation, H, Lp, D)[:, :, :, :L].transpose(0, 3, 1, 2, 4).reshape(B, S, H, D)
    m = m.reshape(B, dilation, H, Lp)[..., :L].transpose(0, 3, 1, 2).reshape(B, S, H)
    l = l.reshape(B, dilation, H, Lp)[..., :L].transpose(0, 3, 1, 2).reshape(B, S, H)
    return acc, m, l


def pool_mixer(u, pool_w, pool_scale):
    B, S, _ = u.shape
    uf = u.astype(jnp.float32).reshape(B, S, N_POOL_GROUPS, POOL_GROUP_DIM)
    c = jnp.cumsum(uf, axis=1)
    t = jnp.arange(S)
    outs = []
    for g, w in enumerate(POOL_WINDOWS):
        cg = c[:, :, g]
        shifted = jnp.pad(cg, ((0, 0), (w, 0), (0, 0)))[:, :S]
        count = jnp.minimum(t + 1, w).astype(jnp.float32)[None, :, None]
        outs.append((cg - shifted) / count - uf[:, :, g])
    d = jnp.stack(outs, axis=2).astype(u.dtype)
    y = jnp.einsum('bsgc,gce->bsge', d, pool_w).reshape(B, S, POOL_W)
    return y * pool_scale


def parallel_mixer(xn, cos, sin, w_in, q_norm_g, k_norm_g, pool_w, pool_scale, w_out):
    B, S, _ = xn.shape
    proj = xn @ w_in
    q, k, v, u = jnp.split(proj, [Q_W, Q_W + KV_W, Q_W + 2 * KV_W], axis=-1)
    q = q.reshape(B, S, N_DIL_GROUPS, ATT_HEADS, HEAD_DIM)
    k = k.reshape(B, S, ATT_HEADS, HEAD_DIM)
    v = v.reshape(B, S, ATT_HEADS, HEAD_DIM)
    q = rope_partial(rms_norm(q, q_norm_g), cos, sin) * (HEAD_DIM ** -0.5)
    k = rope_partial(rms_norm(k, k_norm_g), cos, sin)
    accs, ms, ls = [], [], []
    for g, (window, dilation) in enumerate(DILATED_PAIRS):
        a, m, l = dilated_branch(q[:, :, g], k, v, window, dilation)
        accs.append(a); ms.append(m); ls.append(l)
    ms = jnp.stack(ms)
    wts = jnp.exp(ms - jnp.max(ms, axis=0, keepdims=True))
    num = jnp.sum(wts[..., None] * jnp.stack(accs), axis=0)
    den = jnp.sum(wts * jnp.stack(ls), axis=0)
    attn = (num / den[..., None]).astype(xn.dtype).reshape(B, S, KV_W)
    pooled = pool_mixer(u, pool_w, pool_scale)
    return jnp.concatenate([attn, pooled], axis=-1) @ w_out


def memory_cross_attention(hn, mem_n, w_cq, w_ckv, cq_norm_g, ck_norm_g, w_co):
    B, S, _ = hn.shape
    M = mem_n.shape[1]
    q = (hn @ w_cq).reshape(B, S, X_HEADS, HEAD_DIM)
    k, v = jnp.split(mem_n @ w_ckv, 2, axis=-1)
    k = k.reshape(B, M, X_HEADS, HEAD_DIM)
    v = v.reshape(B, M, X_HEADS, HEAD_DIM)
    q = rms_norm(q, cq_norm_g).astype(jnp.float32) * (HEAD_DIM ** -0.5)
    k = rms_norm(k, ck_norm_g).astype(jnp.float32)
    p = jax.nn.softmax(jnp.einsum('bshd,bmhd->bhsm', q, k), axis=-1)
    o = jnp.einsum('bhsm,bmhd->bshd', p, v.astype(jnp.float32)).astype(hn.dtype)
    return o.reshape(B, S, X_W) @ w_co


def swiglu_ffn(hn, w_gate_up, w_down):
    g, u = jnp.split(hn @ w_gate_up, 2, axis=-1)
    return (jax.nn.silu(g) * u) @ w_down


def setup_inputs(seed: int = 0) -> dict:
    key = jax.random.key(seed)
    ks = jax.random.split(key, 24)
    f32 = jnp.float32

    def w(k, shape, fan_in):
        return jax.random.normal(k, shape, f32) * (fan_in ** -0.5)

    def gain(k, shape):
        return 1.0 + 0.02 * jax.random.normal(k, shape, f32)

    x = jax.random.normal(ks[0], (BATCH, SEQ, D_MODEL), f32)
    mem = jax.random.normal(ks[1], (BATCH, N_MEM, D_MODEL), f32)
    offset = jax.random.randint(ks[2], (BATCH, 1), 0, 4096, dtype=jnp.int32)
    positions = jnp.arange(SEQ, dtype=jnp.int32)[None, :] + offset
    return {
        "x": x,
        "mem": mem,
        "positions": positions,
        "mix_norm_g": gain(ks[3], (DEPTH, D_MODEL)),
        "w_in": w(ks[4], (DEPTH, D_MODEL, IN_W), D_MODEL),
        "q_norm_g": gain(ks[5], (DEPTH, HEAD_DIM)),
        "k_norm_g": gain(ks[6], (DEPTH, HEAD_DIM)),
        "pool_w": w(ks[7], (DEPTH, N_POOL_GROUPS, POOL_GROUP_DIM, POOL_GROUP_DIM), POOL_GROUP_DIM),
        "pool_scale": gain(ks[8], (DEPTH, POOL_W)),
        "w_out": w(ks[9], (DEPTH, MIX_W, D_MODEL), MIX_W),
        "cross_norm_g": gain(ks[10], (DEPTH, D_MODEL)),
        "mem_norm_g": gain(ks[11], (DEPTH, D_MODEL)),
        "w_cq": w(ks[12], (DEPTH, D_MODEL, X_W), D_MODEL),
        "w_ckv": w(ks[13], (DEPTH, D_MODEL, 2 * X_W), D_MODEL),
        "cq_norm_g": gain(ks[14], (DEPTH, HEAD_DIM)),
        "ck_norm_g": gain(ks[15], (DEPTH, HEAD_DIM)),
        "w_co": w(ks[16], (DEPTH, X_W, D_MODEL), X_W),
        "ffn_norm_g": gain(ks[17], (DEPTH, D_MODEL)),
        "w_gate_up": w(ks[18], (DEPTH, D_MODEL, 2 * D_FF), D_MODEL),
        "w_down": w(ks[19], (DEPTH, D_FF, D_MODEL), D_FF),
    }


def reference(x, mem, positions, mix_norm_g, w_in, q_norm_g, k_norm_g, pool_w, pool_scale, w_out,
              cross_norm_g, mem_norm_g, w_cq, w_ckv, cq_norm_g, ck_norm_g, w_co,
              ffn_norm_g, w_gate_up, w_down):
    inv_freq = ROPE_THETA ** (-jnp.arange(0, ROT_DIM, 2, dtype=jnp.float32) / ROT_DIM)
    ang = positions.astype(jnp.float32)[..., None] * inv_freq
    cos, sin = jnp.cos(ang), jnp.sin(ang)
    h = x
    for layer in range(DEPTH):
        h = h + parallel_mixer(rms_norm(h, mix_norm_g[layer]), cos, sin, w_in[layer],
                               q_norm_g[layer], k_norm_g[layer], pool_w[layer],
                               pool_scale[layer], w_out[layer])
        h = h + memory_cross_attention(rms_norm(h, cross_norm_g[layer]),
                                       rms_norm(mem, mem_norm_g[layer]), w_cq[layer],
                                       w_ckv[layer], cq_norm_g[layer], ck_norm_g[layer],
                                       w_co[layer])
        h = h + swiglu_ffn(rms_norm(h, ffn_norm_g[layer]), w_gate_up[layer], w_down[layer])
    return h
```

```python
import contextlib
import numpy as np
import concourse.bass as bass
import concourse.mybir as mybir
from concourse.bass_utils import run_bass_kernel_spmd

F32 = mybir.dt.float32
BF16 = mybir.dt.bfloat16
I32 = mybir.dt.int32
AF = mybir.ActivationFunctionType
ALU = mybir.AluOpType

EPOCH = 12000
NDMASEM = {"sp": 20, "pool": 6, "act": 4, "pe": 1, "dve": 1}

N_CORES = 8
TOK = 2048
HALO = 2048
D = 1024
DFF = 2816
PI = float(np.pi)


class Op:
    __slots__ = ("eng", "emit", "deps", "needs_inc", "is_dma", "sem", "val", "prev_val")

    def __init__(self, eng, emit, is_dma):
        self.eng = eng
        self.emit = emit
        self.deps = []
        self.needs_inc = False
        self.is_dma = is_dma
        self.sem = None
        self.val = 0
        self.prev_val = 0


class Prog:
    ENGS = ("pe", "act", "dve", "pool", "sp")

    def __init__(self, nc):
        self.nc = nc
        self.streams = {e: [] for e in self.ENGS}
        self.last_w = {}
        self.readers = {}

    def _add(self, eng, emit, reads, writes, is_dma):
        o = Op(eng, emit, is_dma)
        psr = [r for r in reads if isinstance(r, tuple) and r and r[0] in ("ps", "pss")]
        if psr:
            writes = list(writes) + psr
        deps = {}
        for r in reads:
            w = self.last_w.get(r)
            if w is not None:
                deps[id(w)] = w
        for k in writes:
            w = self.last_w.get(k)
            if w is not None:
                deps[id(w)] = w
            for rd in self.readers.get(k, ()):
                deps[id(rd)] = rd
        for r in reads:
            lst = self.readers.setdefault(r, [])
            if lst and lst[-1].eng == eng and not lst[-1].is_dma and not is_dma:
                lst[-1] = o
            else:
                lst.append(o)
        for k in writes:
            self.last_w[k] = o
            self.readers[k] = []
        for d in deps.values():
            if d is o:
                continue
            if d.eng == "pe" and eng == "pe" and not d.is_dma and not is_dma:
                continue
            o.deps.append(d)
            d.needs_inc = True
        self.streams[eng].append(o)
        return o

    def op(self, eng, emit, reads=(), writes=()):
        return self._add(eng, emit, reads, writes, False)

    def dma(self, eng, emit, reads=(), writes=()):
        o = self._add(eng, emit, reads, writes, True)
        o.needs_inc = True
        return o

    def barrier(self):
        lasts = []
        for e in self.ENGS:
            st = self.streams[e]
            for o in reversed(st):
                if not o.is_dma and o.emit is not None:
                    lasts.append(o)
                    break
            cnt = 0
            for o in reversed(st):
                if o.is_dma:
                    lasts.append(o)
                    cnt += 1
                    if cnt >= NDMASEM[e]:
                        break
        for e in self.ENGS:
            o = Op(e, None, False)
            for d in lasts:
                if d.eng == "pe" and e == "pe" and not d.is_dma:
                    continue
                o.deps.append(d)
                d.needs_inc = True
            self.streams[e].append(o)

    def sync_on(self, src_engs, dst_engs):
        lasts = []
        for e in src_engs:
            st = self.streams[e]
            for o in reversed(st):
                if not o.is_dma and o.emit is not None:
                    lasts.append(o)
                    break
            cnt = 0
            for o in reversed(st):
                if o.is_dma:
                    lasts.append(o)
                    cnt += 1
                    if cnt >= NDMASEM[e]:
                        break
        for e in dst_engs:
            o = Op(e, None, False)
            for d in lasts:
                if d.eng == "pe" and e == "pe" and not d.is_dma:
                    continue
                o.deps.append(d)
                d.needs_inc = True
            self.streams[e].append(o)

    def finalize_and_emit(self, final_wait_ops):
        nc = self.nc
        n_inc = {e: 0 for e in self.ENGS}
        for e in self.ENGS:
            for o in self.streams[e]:
                if not o.is_dma and o.needs_inc:
                    n_inc[e] += 1
        with contextlib.ExitStack() as es:
            eng_sems = {}
            for e in self.ENGS:
                ne = max(1, -(-n_inc[e] // EPOCH))
                eng_sems[e] = [es.enter_context(nc.semaphore(f"s_{e}_{i}")) for i in range(ne)]
            dma_sems = {}
            for e in self.ENGS:
                dma_sems[e] = [es.enter_context(nc.semaphore(f"d_{e}_{i}")) for i in range(NDMASEM[e])]
            for e in self.ENGS:
                c = 0
                dcount = [0] * NDMASEM[e]
                nd = 0
                for o in self.streams[e]:
                    if o.is_dma:
                        si = nd % NDMASEM[e]
                        nd += 1
                        o.sem = dma_sems[e][si]
                        o.prev_val = dcount[si]
                        dcount[si] += 16
                        o.val = dcount[si]
                    elif o.needs_inc:
                        o.sem = eng_sems[e][c // EPOCH]
                        o.val = c % EPOCH + 1
                        c += 1
            stats = {}

            def run_stream(e, eh):
                known = {}
                nw = 0
                for o in self.streams[e]:
                    waits = {}
                    for d in o.deps:
                        key = id(d.sem)
                        if known.get(key, 0) >= d.val:
                            continue
                        if key not in waits or waits[key][1] < d.val:
                            waits[key] = (d.sem, d.val)
                    if o.is_dma and o.prev_val > 0:
                        key = id(o.sem)
                        if known.get(key, 0) < o.prev_val:
                            if key not in waits or waits[key][1] < o.prev_val:
                                waits[key] = (o.sem, o.prev_val)
                    for key, (s, v) in waits.items():
                        eh.wait_ge(s, v)
                        known[key] = v
                        nw += 1
                    if o.emit is None:
                        continue
                    ins = o.emit(eh)
                    if o.is_dma:
                        ins.then_inc(o.sem, 16)
                    elif o.needs_inc:
                        ins.then_inc(o.sem, 1)
                if e == "sp":
                    for o in final_wait_ops:
                        if known.get(id(o.sem), 0) < o.val:
                            eh.wait_ge(o.sem, o.val)
                            known[id(o.sem)] = o.val
                stats[e] = (len(self.streams[e]), nw)

            with nc.Block() as block:
                @block.tensor
                def _(eh):
                    run_stream("pe", eh)

                @block.scalar
                def _(eh):
                    run_stream("act", eh)

                @block.vector
                def _(eh):
                    run_stream("dve", eh)

                @block.gpsimd
                def _(eh):
                    run_stream("pool", eh)

                @block.sync
                def _(eh):
                    run_stream("sp", eh)
            self.stats = stats


ARENA_BYTES = 207 * 1024
KB = 1024
C_MIX, C_CROSS, C_MEM, C_FFN = 0, 8, 16, 24
C_Q, C_K, C_CQ, C_CK, C_PS, C_INVF = 32, 33, 34, 35, 36, 40
C_QS, C_CQS, C_EPS = 41, 42, 43
NCOL = 48
CB_ID, CB_RM, CB_MS, CB_MF, NCB = 0, 128, 256, 512, 768

DILS = (1, 4, 16)
DBG = {"stop": None}


class _Stop(Exception):
    pass


def build_program():
    try:
        return _build_program_inner()
    except _Stop as ex:
        return ex.nc


def _build_program_inner():
    nc = bass.Bass("TRN2", target_bir_lowering=False)
    dt = lambda name, shape, dtp, kind: nc.dram_tensor(name, shape, dtp, kind=kind).ap()
    xT = dt("xT", [D, HALO + TOK], F32, "ExternalInput")
    posb_d = dt("posb", [128, 512], I32, "ExternalInput")
    memT = dt("memT", [D, 256], F32, "ExternalInput")
    cols_d = dt("cols", [128, NCOL], F32, "ExternalInput")
    cb_d = dt("cb", [128, NCB], F32, "ExternalInput")
    cnt_d = dt("cntinv", [128, 64], F32, "ExternalInput")
    w_in_d = dt("w_in_t", [6, 128, 4096], F32, "ExternalInput")
    w_out_d = dt("w_out_t", [128, 8192], F32, "ExternalInput")
    w_cq_d = dt("w_cq_t", [128, 4096], F32, "ExternalInput")
    w_ckv_d = dt("w_ckv_t", [128, 8192], F32, "ExternalInput")
    w_co_d = dt("w_co_t", [128, 4096], F32, "ExternalInput")
    w_gu_d = dt("w_gu_t", [11, 128, 4096], F32, "ExternalInput")
    w_d_d = dt("w_d_t", [4, 128, 22 * 256], F32, "ExternalInput")
    pw_d = dt("pool_w_t", [128, 512], F32, "ExternalInput")
    outT = dt("outT", [D, TOK], F32, "ExternalOutput")
    dbg_d = dt("dbg", [128, 16384], F32, "ExternalOutput") if DBG["stop"] is not None else None

    es = contextlib.ExitStack()
    with es:
        arena = es.enter_context(nc.sbuf_tensor("arena", [128, ARENA_BYTES // 4], F32))
        psb = [es.enter_context(nc.psum_tensor(f"ps{i}", [128, 512], F32)) for i in range(8)]
        P = Prog(nc)

        def view(off, nelem, dtp):
            sz = 2 if dtp == BF16 else 4
            assert off % 32 == 0, off
            nb = nelem * sz
            assert nb % 4 == 0 and off + nb <= ARENA_BYTES, (off, nb)
            v = arena[:, off // 4:(off + nb) // 4]
            if dtp != F32:
                v = v.bitcast(dtp)
            return v

        def ap2(v, p0, npart, col0, dims):
            pstride = v.ap[0][0]
            return bass.AP(v.tensor, v.offset + p0 * pstride + col0, [[pstride, npart]] + [list(d) for d in dims])

        def MM(out, lhsT, rhs, start, stop, r, w):
            P.op("pe", lambda e: e.matmul(out, lhsT=lhsT, rhs=rhs, start=start, stop=stop), r, w)

        def TR(out, in_, ident, r, w):
            P.op("pe", lambda e: e.transpose(out=out, in_=in_, identity=ident), r, w)

        def ACT(out, in_, func, r, w, bias=None, scale=None):
            kw = {}
            if bias is not None:
                kw["bias"] = bias
            if scale is not None:
                kw["scale"] = scale
            P.op("act", lambda e: e.activation(out=out, in_=in_, func=func, **kw), r, w)

        def TT(eng, out, in0, in1, op, r, w):
            P.op(eng, lambda e: e.tensor_tensor(out=out, in0=in0, in1=in1, op=op), r, w)

        def TS(eng, out, in0, s1, s2, op0, op1, r, w):
            if op1 is None:
                P.op(eng, lambda e: e.tensor_scalar(out=out, in0=in0, scalar1=s1, scalar2=None, op0=op0), r, w)
            else:
                P.op(eng, lambda e: e.tensor_scalar(out=out, in0=in0, scalar1=s1, scalar2=s2, op0=op0, op1=op1), r, w)

        def STT(out, in0, scalar, in1, op0, op1, r, w):
            P.op("dve", lambda e: e.scalar_tensor_tensor(out=out, in0=in0, scalar=scalar, in1=in1, op0=op0, op1=op1), r, w)

        def CP(eng, out, in_, r, w):
            if eng == "act":
                ACT(out, in_, AF.Copy, r, w)
            else:
                P.op(eng, lambda e: e.tensor_copy(out=out, in_=in_), r, w)

        def RCP(out, in_, r, w):
            P.op("dve", lambda e: e.reciprocal(out=out, in_=in_), r, w)

        def MEMSET(eng, out, val, w):
            P.op(eng, lambda e: e.memset(out, val), (), w)

        def DMA(q, out, in_, r, w, **kw):
            return P.dma(q, lambda e: e.dma_start(out=out, in_=in_, **kw), r, w)

        def WLOAD(out, in_, w):
            return P.dma("pool", lambda e: e.dma_start(out=out, in_=in_, max_dma_last_dim=4096), (), w)

        def stop_here(k, dumps):
            if DBG["stop"] != k:
                return
            P.barrier()
            ops = []
            c0 = 0
            for (v, n) in dumps:
                ops.append(P.dma("pool", lambda e, v=v, c0=c0, n=n: e.dma_start(out=dbg_d[:, c0:c0 + n], in_=v[:, 0:n], max_dma_last_dim=4096), (), ()))
                c0 += n
            P.finalize_and_emit(ops)
            ex = _Stop()
            ex.nc = nc
            raise ex

        cols = view(0, NCOL, F32)
        cb = view(256, NCB, BF16)
        ones = view(1792, 128, BF16)
        cnt = view(2048, 64, F32)
        poolw = view(2304, 512, BF16)
        ident = cb[:, CB_ID:CB_ID + 128]
        Rm = cb[:, CB_RM:CB_RM + 128]
        maskS = cb[:, CB_MS:CB_MS + 256]
        maskF = cb[:, CB_MF:CB_MF + 256]
        sin_all = view(4 * KB, 512, F32)
        cos_all = view(6 * KB, 512, F32)

        def col(i, p0=0, n=128):
            return cols[p0:p0 + n, i:i + 1]

        R_X, R_Y, R_C, R_D, R_E, R_F, R_W, R_G = 8 * KB, 40 * KB, 72 * KB, 104 * KB, 136 * KB, 152 * KB, 168 * KB, 184 * KB
        xn_halo = view(R_X, 8 * 2048, BF16)
        xn_own = view(R_Y, 8 * 2048, BF16)
        KT = view(R_C, 4 * 4096, BF16)
        VT = view(R_D, 4 * 4096, BF16)
        QT_a = view(R_X, 8 * 2048, BF16)
        QT_b = view(R_E, 4 * 2048, BF16)
        pooled = view(R_F, 4 * 2048, BF16)
        wbuf = [view(R_W, 4096, BF16), view(R_W + 8 * KB, 4096, BF16)]
        attnT = view(R_G, 4 * 2048, BF16)

        def xn_blk(b, kc):
            v = xn_halo if b < 4 else xn_own
            bb = b % 4
            return v[:, kc * 2048 + bb * 512: kc * 2048 + bb * 512 + 512]

        def qt_head(qh):
            if qh < 8:
                return QT_a[:, qh * 2048:(qh + 1) * 2048]
            return QT_b[:, (qh - 8) * 2048:(qh - 7) * 2048]

        DMA("sp", cols[:, 0:41], cols_d[:, 0:41], (), ["cols"])
        DMA("sp", cnt, cnt_d, (), ["cnt"])
        WLOAD(cb, cb_d, ["cb"])
        WLOAD(poolw, pw_d, ["poolw"])
        MEMSET("pool", ones, 1.0, ["ones"])
        MEMSET("pool", cols[:, C_EPS:C_EPS + 1], 1e-6, ["cols_eps"])
        sc = float(128.0 ** -0.5)
        TS("dve", cols[:, C_QS:C_QS + 1], cols[:, C_Q:C_Q + 1], sc, None, ALU.mult, None, ["cols"], ["cols_qs"])
        TS("dve", cols[:, C_CQS:C_CQS + 1], cols[:, C_CQ:C_CQ + 1], sc, None, ALU.mult, None, ["cols"], ["cols_qs"])
        CONSTS = ["cols", "cols_eps", "cols_qs", "cb", "ones", "cnt", "poolw"]

        posb = view(R_G + 2 * KB, 512, I32)
        DMA("sp", posb, posb_d, (), ["posb"])
        WLOAD(wbuf[1], w_in_d[1], [("wbuf", 1)])
        P.dma("pool", lambda e: e.dma_start(out=wbuf[0], in_=w_in_d[2], max_dma_last_dim=4096), [("wbuf", 1)], [("wbuf", 0)])
        xst = [view(R_E, 4096, F32), view(R_F, 4096, F32)]
        xT_v = xT.rearrange("(k p) t -> p k t", p=128)
        DMA("sp", xst[0].rearrange("p (k t) -> p k t", k=8), xT_v[:, :, 0:512], (), [("xst", 0)])
        DMA("sp", xst[1].rearrange("p (k t) -> p k t", k=8), xT_v[:, :, 512:1024], [("xst", 0)], [("xst", 1)])
        def emit_trig():
          ang = view(R_G + 4 * KB, 512, F32)
          kf = view(R_G + 6 * KB, 512, F32)
          ki = view(R_G + 8 * KB, 512, I32)
          ang2 = view(R_G + 10 * KB, 512, F32)
          TS("dve", ang, posb, col(C_INVF), None, ALU.mult, None, ["posb", "cols"], ["ang"])
          C1 = 6.28125
          C2 = float(2 * np.pi - 6.28125)
          TS("dve", kf, ang, float(1.0 / (2 * np.pi)), None, ALU.mult, None, ["ang"], ["kf"])
          CP("dve", ki, kf, ["kf"], ["ki"])
          CP("dve", kf, ki, ["ki"], ["kf"])
          STT(ang, kf, -C1, ang, ALU.mult, ALU.add, ["kf", "ang"], ["ang"])
          STT(ang, kf, -C2, ang, ALU.mult, ALU.add, ["kf", "ang"], ["ang"])
          TS("dve", kf, ang, PI, -2 * PI, ALU.is_gt, ALU.mult, ["ang"], ["kf"])
          TT("dve", ang, ang, kf, ALU.add, ["ang", "kf"], ["ang"])
          TS("dve", kf, ang, -PI, 2 * PI, ALU.is_lt, ALU.mult, ["ang"], ["kf"])
          TT("dve", ang, ang, kf, ALU.add, ["ang", "kf"], ["ang"])
          ACT(sin_all, ang, AF.Sin, ["ang"], ["sin_all"])
          TS("dve", ang2, ang, PI / 2, None, ALU.add, None, ["ang"], ["ang2"])
          TS("dve", kf, ang2, PI, -2 * PI, ALU.is_gt, ALU.mult, ["ang2"], ["kf"])
          TT("dve", ang2, ang2, kf, ALU.add, ["ang2", "kf"], ["ang2"])
          ACT(cos_all, ang2, AF.Sin, ["ang2", "ang", "kf", "ki", "posb"],
              ["cos_all", ("sdr", 0), ("sdr", 1), ("t1", 0), ("t1", 1), "t2"])

        SCR = R_G
        sqb = [view(SCR + i * KB, 512, BF16) for i in range(2)]
        sdr = [view(SCR + 2 * KB + i * 2 * KB, 512, F32) for i in range(2)]
        t1b = [view(SCR + 6 * KB + i * 2 * KB, 512, F32) for i in range(2)]
        t2b = [view(SCR + 10 * KB, 512, F32)]
        CSb = [(view(SCR + 12 * KB, 512, F32), view(SCR + 14 * KB, 512, F32)),
               (view(SCR + 16 * KB, 512, F32), view(SCR + 18 * KB, 512, F32))]
        rsq = [view(SCR + 16 * KB + i * KB, 512, BF16) for i in range(4)]
        rsd = view(SCR + 20 * KB, 512, F32)
        RMS_BANK = 7

        def rms_steps(src_kc, dst_kc, gcol0, srckeys, tagdst, bank=RMS_BANK, sq_eng="act"):
            steps = []
            for kc in range(8):
                def f(kc=kc):
                    if sq_eng == "act":
                        ACT(rsq[kc % 4], src_kc(kc), AF.Square, srckeys, [("rsq", kc % 4)])
                    else:
                        TT(sq_eng, rsq[kc % 4], src_kc(kc), src_kc(kc), ALU.mult, srckeys, [("rsq", kc % 4)])
                steps.append(f)
            for kc in range(8):
                def f2(kc=kc):
                    MM(psb[bank][:, :], ones, rsq[kc % 4], kc == 0, kc == 7, [("rsq", kc % 4), "ones"], [("ps", bank)])
                steps.append(f2)

            def g_():
                ACT(rsd, psb[bank][:, :], AF.Ln, [("ps", bank), "cols_eps"], ["rsd"], bias=col(C_EPS), scale=1.0 / D)
                ACT(rsd, rsd, AF.Exp, ["rsd"], ["rsd"], scale=-0.5)
            steps.append(g_)
            for kc in range(8):
                def h_(kc=kc):
                    td = tagdst(kc)
                    STT(dst_kc(kc), src_kc(kc), col(gcol0 + kc), rsd, ALU.mult, ALU.mult,
                        list(srckeys) + ["rsd", "cols"], td if isinstance(td, list) else [td])
                steps.append(h_)
            return steps

        def rms_block(src_kc, dst_kc, gcol0, srckeys, tagdst, uid=0, ps_bank=RMS_BANK):
            st_ = rms_steps(src_kc, dst_kc, gcol0, srckeys, tagdst, bank=ps_bank)
            order = [0, 1, 2, 3, 8, 9, 10, 11, 4, 5, 6, 7, 12, 13, 14, 15] + list(range(16, 25))
            for k_ in order:
                st_[k_]()

        cs_state = {"n": 0, "nbuf": 1}

        def load_cs(b):
            i = cs_state["n"] % cs_state["nbuf"]
            cs_state["n"] += 1
            Cb, Sb = CSb[i]
            for half in range(2):
                DMA("sp", Cb[16 * half:16 * half + 16, :], cos_all[16 * b:16 * b + 16, :], ["cos_all"], [("cs", i)])
                DMA("sp", Sb[16 * half:16 * half + 16, :], sin_all[16 * b:16 * b + 16, :], ["sin_all"], [("cs", i)])
            return i

        def x_rms_steps(b):
            xs = xst[b % 2]
            return rms_steps(lambda kc, xs=xs: xs[:, kc * 512:(kc + 1) * 512], lambda kc, b=b: xn_blk(b, kc), C_MIX,
                             [("xst", b % 2)], lambda kc, b=b: ("xn", b, kc), sq_eng="act")

        def run_units(units, wsel, dest_fn, gcol_fn, kind_fn, hooks, two_rq=False, hooks_post=None):
            cs_of = {}

            def st0(u, n):
                b, h, tag = u
                if kind_fn(tag) == "qk" and (b, tag) not in cs_of:
                    cs_of[(b, tag)] = load_cs(b)
                wb, wkey = wsel(tag)
                bank = n % 4
                for kc in range(8):
                    MM(psb[bank][:, :], wb[:, kc * 512 + h * 128: kc * 512 + h * 128 + 128], xn_blk(b, kc),
                       kc == 0, kc == 7, [wkey, ("xn", b, kc)], [("ps", bank)])

            def st1a(u, n):
                b, h, tag = u
                bank = n % 4
                dest, dkey = dest_fn(b, h, tag)
                if kind_fn(tag) == "v":
                    CP("act" if n % 2 == 0 else "dve", dest, psb[bank][:, :], [("ps", bank)], [dkey])
                    return
                ACT(sqb[n % 2], psb[bank][:, :], AF.Square, [("ps", bank)], [("sq", n % 2)])

            def st1(u, n):
                b, h, tag = u
                bank = n % 4
                dest, dkey = dest_fn(b, h, tag)
                if kind_fn(tag) == "v":
                    return
                s_ = sqb[n % 2]
                sb = 4 + n % 2
                MM(psb[sb][:, :], ones, s_, True, True, [("sq", n % 2), "ones"], [("ps", sb)])
                sd = sdr[n % 2]
                ACT(sd, psb[sb][:, :], AF.Ln, [("ps", sb), "cols_eps"], [("sdr", n % 2)], bias=col(C_EPS), scale=1.0 / 128)
                ACT(sd, sd, AF.Exp, [("sdr", n % 2)], [("sdr", n % 2)], scale=-0.5)
                STT(dest, psb[bank][:, :], col(gcol_fn(tag)), sd, ALU.mult, ALU.mult,
                    [("ps", bank), ("sdr", n % 2), "cols", "cols_qs"], [dkey])

            def st2a(u, n):
                b, h, tag = u
                if kind_fn(tag) == "v":
                    return
                csi = cs_of[(b, tag)]
                dest, dkey = dest_fn(b, h, tag)
                Cb, Sb = CSb[csi]
                rb = 6 + (n % 2 if two_rq else 0)
                MM(psb[rb][:, :], Rm, dest, True, True, [dkey, "cb"], [("ps", rb)])
                t1 = t1b[n % 2]
                t2 = t2b[0]
                TT("dve", t1[0:32, :], psb[rb][0:32, :], Sb[0:32, :], ALU.mult, [("ps", rb), ("cs", csi)], [("t1", n % 2)])
                TT("pool", t2[0:32, :], dest[0:32, :], Cb[0:32, :], ALU.mult, [dkey, ("cs", csi)], ["t2"])

            def st2b(u, n):
                b, h, tag = u
                if kind_fn(tag) == "v":
                    return
                dest, dkey = dest_fn(b, h, tag)
                TT("dve", dest[0:32, :], t1b[n % 2][0:32, :], t2b[0][0:32, :], ALU.add, [("t1", n % 2), "t2"], [dkey])

            n = len(units)
            for i in range(n + 4):
                if i < n:
                    st0(units[i], i)
                    for f in hooks.get(i, ()):
                        f()
                if 0 <= i - 1 < n:
                    st1a(units[i - 1], i - 1)
                if 0 <= i - 4 < n:
                    st2a(units[i - 4], i - 4)
                if 0 <= i - 2 < n:
                    st1(units[i - 2], i - 2)
                if 0 <= i - 4 < n:
                    st2b(units[i - 4], i - 4)
                if hooks_post is not None:
                    for f in hooks_post.get(i, ()):
                        f()

        wu = view(R_F, 4096, BF16)
        st_ = x_rms_steps(0)
        for k_ in [0, 1, 2, 3, 8, 9, 10, 11, 4, 5, 6, 7, 12, 13, 14, 15] + list(range(16, 25)):
            st_[k_]()
        emit_trig()
        stop_here(0, [(sin_all, 512), (cos_all, 512)])
        units = []
        hooks = {}
        hooks_post = {}
        for b in range(8):
            base = len(units)
            for h in range(4):
                units.append((b, h, "k"))
            for h in range(4):
                units.append((b, h, "v"))
            if b + 1 < 8:
                nb_ = b + 1
                steps = x_rms_steps(nb_)
                per = [[] for _ in range(8)]
                k0 = 0
                n2 = b + 2
                if n2 < 8:
                    per[0].append(lambda n2=n2: DMA("sp", xst[n2 % 2].rearrange("p (k t) -> p k t", k=8),
                                                    xT_v[:, :, 512 * n2:512 * n2 + 512], (), [("xst", n2 % 2)]))
                sq = steps[k0:k0 + 8]; mmr = steps[k0 + 8:k0 + 16]; ln = steps[k0 + 16]; st = steps[k0 + 17:k0 + 25]
                post = [[] for _ in range(8)]
                for j in range(8):
                    per[j // 2].append(sq[j])
                    post[j // 2 + 1].append(mmr[j])
                per[5].append(ln)
                for j in range(8):
                    per[5 + j // 3].append(st[j])
                for j in range(8):
                    hooks[base + j] = per[j]
                    hooks_post[base + j] = post[j]
            else:
                hooks[base] = [lambda: WLOAD(wu, w_in_d[0], [("xst", 1), "wu"])]
        kv_dest = lambda b, h, tag: ((KT if tag == "k" else VT)[:, h * 4096 + b * 512: h * 4096 + b * 512 + 512],
                                     ("KT" if tag == "k" else "VT", h, b))
        run_units(units, lambda tag: (wbuf[1], ("wbuf", 1)) if tag == "k" else (wbuf[0], ("wbuf", 0)),
                  kv_dest, lambda tag: C_K, lambda tag: "qk" if tag == "k" else "v", hooks, hooks_post=hooks_post)
        for ch in range(4):
            for kc in range(8):
                MM(psb[4][:, ch * 16:(ch + 1) * 16], wu[:, kc * 512 + ch * 128: kc * 512 + ch * 128 + 128],
                   xn_blk(3, kc)[:, 496:512], kc == 0, kc == 7, ["wu", ("xn", 3, kc)], [("ps", 4)])
        if DBG["stop"] == 3:
            P.barrier()
        else:
            P.sync_on(["pe", "act", "dve", "pool", "sp"], ["act", "dve", "pool", "sp"])
        stop_here(3, [(KT, 8192), (VT, 4096)])

        NU = 16 + TOK
        WLOAD(wbuf[0], w_in_d[3], [("wbuf", 0)])
        WLOAD(wbuf[1], w_in_d[4], [("wbuf", 1)])
        u_off = [R_X, R_X + 8256, R_X + 2 * 8256, R_E]
        u_g = [view(o, NU, F32) for o in u_off]
        d_g = [view(o, TOK, BF16) for o in u_off]
        tmpA = view(R_G, NU, F32)
        tmpB = view(R_G + 8256, NU, F32)
        t16 = view(R_G + 2 * 8256, 16, F32)
        ev = {"n": 0}
        ubank = {"n": 0}

        def u_proj(ch):
            for bi in range(4):
                b = 4 + bi
                bank = ubank["n"] % 4
                ubank["n"] += 1
                for kc in range(8):
                    MM(psb[bank][:, :], wu[:, kc * 512 + ch * 128: kc * 512 + ch * 128 + 128], xn_blk(b, kc),
                       kc == 0, kc == 7, ["wu", ("xn", b, kc)], [("ps", bank)])
                CP("act", u_g[ch][:, 16 + bi * 512: 16 + bi * 512 + 512],
                   psb[bank][:, :], [("ps", bank)], [("u", ch)])
                ev["n"] += 1
            CP("act", u_g[ch][:, 0:16], psb[4][:, ch * 16:(ch + 1) * 16], [("ps", 4)], [("u", ch)])

        def pooling(g):
            ug = u_g[g]
            cur, ckey = ug, ("u", g)
            vfrom = 0
            bufs = [tmpA, tmpB]
            for lvl in range(g + 1):
                s_ = 1 << lvl
                nxt = bufs[lvl % 2]
                nkey = ("ptmp", lvl % 2)
                v2 = vfrom + s_
                TT("dve", nxt[:, v2:NU], cur[:, v2:NU], cur[:, v2 - s_:NU - s_], ALU.add, [ckey], [nkey])
                cur, ckey, vfrom = nxt, nkey, v2
            w = float(1 << (g + 1))
            TT("dve", t16, cur[:, 16:32], cnt[:, g * 16:(g + 1) * 16], ALU.mult, [ckey, "cnt"], ["t16"])
            TT("dve", t16, t16, ug[:, 16:32], ALU.subtract, ["t16", ("u", g)], ["t16"])
            STT(d_g[g], cur[:, 16:NU], 1.0 / w, ug[:, 16:NU], ALU.mult, ALU.subtract,
                [ckey, ("u", g), "t16"], [("u", g), ("d", g)])
            CP("dve", d_g[g][:, 0:16], t16, ["t16", ("d", g)], [("d", g), ("u", g)])

        def pool_mm(g):
            for tb in range(4):
                bank = ubank["n"] % 4
                ubank["n"] += 1
                MM(psb[bank][:, :], poolw[:, g * 128:(g + 1) * 128], d_g[g][:, tb * 512: tb * 512 + 512],
                   True, True, ["poolw", ("d", g)], [("ps", bank)])
                ACT(pooled[:, g * TOK + tb * 512: g * TOK + tb * 512 + 512], psb[bank][:, :], AF.Copy,
                    [("ps", bank), "cols"], [("pooled", g, tb)] + (["wu"] if g < 2 else []), scale=col(C_PS + g))

        u_proj(3); pooling(3)
        u_proj(2); pooling(2)
        u_proj(1); pooling(1)
        pool_mm(3)
        u_proj(0); pooling(0)
        pool_mm(2); pool_mm(1); pool_mm(0)
        if DBG["stop"] == 2:
            P.barrier()
        else:
            P.sync_on(["pe", "act", "dve", "pool", "sp"], ["act", "dve", "pool", "sp"])
        stop_here(2, [(pooled, 8192)])

        cs_state["nbuf"] = 2
        cs_state["n"] = 1
        qunits_ = [(b, h, ("q", g)) for g in range(3) for b in range(4, 8) for h in range(4)]
        qhooks = {16: [lambda: WLOAD(wbuf[0], w_in_d[5], [("wbuf", 0)])]}
        run_units(qunits_, lambda tag: (wbuf[tag[1] % 2], ("wbuf", tag[1] % 2)),
                  lambda b, h, tag: (qt_head(tag[1] * 4 + h)[:, (b - 4) * 512:(b - 4) * 512 + 512], ("QT", tag[1] * 4 + h, b - 4)),
                  lambda tag: C_QS, lambda tag: "qk", qhooks, two_rq=True)
        if DBG["stop"] == 4:
            P.barrier()
        else:
            P.sync_on(["pe", "act", "dve", "pool", "sp"], ["dve", "pool", "sp"])
            P.sync_on(["pe"], ["act"])
        stop_here(4, [(QT_a, 8192)])

        numacc = view(R_Y, TOK, F32)
        denacc = view(R_Y + 8 * KB, TOK, F32)
        vblk = [view(R_Y + 16 * KB, 32 * 128, BF16), view(R_Y + 24 * KB, 32 * 128, BF16)]
        NPT = 12
        PtAll = view(200 * KB, NPT * 256, BF16)
        Pt = [PtAll[:, i * 256:(i + 1) * 256] for i in range(NPT)]
        wout = view(R_W, 8192, BF16)
        WLOAD(wout[:, 0:4096], w_out_d[:, 0:4096], [("wout", 0)])
        WLOAD(wout[:, 4096:8192], w_out_d[:, 4096:8192], [("wout", 1)])
        ev_ctr = {"n": 0}
        LAG = 5
        groups = [(h, g) for h in range(4) for g in range(3)]

        def grp_info(gi):
            h, g = groups[gi]
            dil = DILS[g]
            nq = 16 // dil
            return h, g, dil, nq

        def emit_vblocks(gi, only_batch=None):
            h, g, dil, nq = grp_info(gi)
            VTh = VT[:, h * 4096:(h + 1) * 4096]
            VT_keys = [("VT", h, b_) for b_ in range(8)]
            vbi = gi % 2
            VB = vblk[vbi]
            tiles = [(r, kt) for r in range(dil) for kt in range(nq + 1)]
            for t0 in range(0, len(tiles), 8):
                if only_batch is not None and t0 // 8 != only_batch:
                    continue
                grp = tiles[t0:t0 + 8]
                bank = 7
                pb = psb[bank][:, :].bitcast(BF16)
                for j, (r, kt) in enumerate(grp):
                    c0 = 2048 + r + dil * 128 * (kt - 1)
                    TR(pb[:, j * 128:(j + 1) * 128], VTh[:, c0: c0 + dil * 127 + 1: dil], ident, VT_keys + ["cb"], [("ps", bank)])
                s0 = grp[0][0] * (nq + 1) + grp[0][1]
                ncol = len(grp) * 128
                CP("act" if ev_ctr["n"] % 2 == 0 else "dve", VB[:, s0 * 128: s0 * 128 + ncol], pb[:, 0:ncol],
                   [("ps", bank)], [("vb", vbi, s0 + j) for j in range(len(grp))])
                ev_ctr["n"] += 1

        gunits = []
        for gi in range(len(groups)):
            h, g, dil, nq = grp_info(gi)
            lst = [(r, j) for r in range(dil) for j in range(nq)]
            for k_, (r, j) in enumerate(lst):
                gunits.append((gi, r, j, k_, len(lst)))

        def S_stage(i):
            gi, r, j, k_, ng = gunits[i]
            h, g, dil, nq = grp_info(gi)
            if k_ >= LAG + 1 and (k_ - LAG - 1) % 2 == 0 and gi + 1 < len(groups):
                emit_vblocks(gi + 1, only_batch=(k_ - LAG - 1) // 2)
            KTh = KT[:, h * 4096:(h + 1) * 4096]
            KT_keys = [("KT", h, b_) for b_ in range(8)]
            qh = g * 4 + h
            Qh = qt_head(qh)
            Q_keys = [("QT", qh, b_) for b_ in range(4)]
            slot = i % 3
            sp_ = psb[slot][:, 0:256]
            q0 = r + dil * 128 * j
            qap = Qh[:, q0: q0 + dil * 127 + 1: dil]
            for half in range(2):
                kt = j + half
                c0 = 2048 + r + dil * 128 * (kt - 1)
                MM(sp_[:, half * 128:(half + 1) * 128], KTh[:, c0: c0 + dil * 127 + 1: dil], qap, True, True,
                   KT_keys + Q_keys, [("ps", slot)])
            pslot = i % NPT
            pt = Pt[pslot]
            ACT(pt, sp_, AF.Exp, [("ps", slot)], [("pt", pslot)] + ([("cs", 1)] if i < NPT else []))
            TT("pool" if i % 3 == 2 else "dve", pt, pt, maskF if j == 0 else maskS, ALU.mult, [("pt", pslot), "cb"], [("pt", pslot)])

        def PV_stage(i):
            gi, r, j, k_, ng = gunits[i]
            h, g, dil, nq = grp_info(gi)
            vbi = gi % 2
            VB = vblk[vbi]
            pslot = i % NPT
            pt = Pt[pslot]
            bsel = (k_ // 4) % 2
            nb_, db_ = 3 + bsel, 5 + bsel
            cs = (k_ % 4) * 128
            for half in range(2):
                sl = r * (nq + 1) + j + half
                MM(psb[nb_][:, cs:cs + 128], VB[:, sl * 128:(sl + 1) * 128], pt[:, half * 128:(half + 1) * 128],
                   half == 0, half == 1, [("vb", vbi, sl), ("pt", pslot)], [("ps", nb_)])
            if k_ % 4 == 3:
                s0_ = pslot - 3
                for half in range(2):
                    MM(psb[db_][:, 0:512], ones, ap2(PtAll, 0, 128, s0_ * 256 + half * 128, [[256, 4], [1, 128]]),
                       half == 0, half == 1, ["ones"] + [("pt", s0_ + q_) for q_ in range(4)], [("ps", db_)])
                r0, j0 = gunits[i - 3][1], gunits[i - 3][2]
                if dil == 1:
                    dims = [[1, 512]]
                    c0 = 128 * j0
                elif dil == 4:
                    dims = [[4, 512]]
                    c0 = r0
                else:
                    dims = [[1, 4], [16, 128]]
                    c0 = r0
                for (acc, bank_, key) in ((numacc, nb_, "nacc"), (denacc, db_, "dacc")):
                    oap = ap2(acc, 0, 128, c0, dims)
                    if len(dims) == 2:
                        iap = ap2(psb[bank_][:, :], 0, 128, 0, [[128, 4], [1, 128]])
                    else:
                        iap = psb[bank_][:, :]
                    if g == 0:
                        CP("dve", oap, iap, [("ps", bank_)], [(key, j0 // 4)])
                    else:
                        keys4 = [(key, t_) for t_ in range(4)]
                        TT("dve", oap, iap, oap, ALU.add, [("ps", bank_)] + keys4, keys4)
            if g == 2 and k_ == ng - 1:
                for tb in range(4):
                    def fin(tb=tb, h=h):
                        dv = denacc[:, tb * 512:(tb + 1) * 512]
                        ACT(dv, dv, AF.Ln, [("dacc", tb)], [("dacc", tb)])
                        ACT(dv, dv, AF.Exp, [("dacc", tb)], [("dacc", tb)], scale=-1.0)
                        TT("dve", attnT[:, h * TOK + tb * 512: h * TOK + tb * 512 + 512],
                           numacc[:, tb * 512:(tb + 1) * 512], dv, ALU.mult,
                           [("nacc", tb), ("dacc", tb)], [("attnT", h, tb)])
                    deferred.setdefault(i + LAG + 1 + tb, []).append(fin)

        emit_vblocks(0)
        deferred = {}
        for i in range(len(gunits) + LAG):
            if i < len(gunits):
                S_stage(i)
            if 0 <= i - LAG < len(gunits):
                PV_stage(i - LAG)
            for f in deferred.pop(i, ()):
                f()
        for k_ in sorted(deferred):
            for f in deferred[k_]:
                f()
        hT = view(R_X, 8 * TOK, F32)
        hT3 = hT.rearrange("p (k t) -> p k t", k=8)
        memf = view(R_E, 8 * 256, F32)
        early_x = DBG["stop"] != 5
        if early_x:
            DMA("sp", hT3[:, 0:4, 0:512], xT_v[:, 0:4, HALO:HALO + 512], (),
                [("hT", oc, 0) for oc in range(4)] + [("QT", qh, b_) for qh in range(8) for b_ in range(4)])
            DMA("sp", memf.rearrange("p (k t) -> p k t", k=8), memT.rearrange("(k p) t -> p k t", p=128), (),
                ["memf"] + [("QT", qh, b_) for qh in range(8, 12) for b_ in range(4)])
        if DBG["stop"] == 5:
            P.barrier()
        else:
            P.sync_on(["pe", "act", "dve", "pool", "sp"], ["act", "dve", "pool", "sp"])
        stop_here(5, [(attnT, 8192)])

        if not early_x:
            DMA("sp", memf.rearrange("p (k t) -> p k t", k=8), memT.rearrange("(k p) t -> p k t", p=128), (), ["memf"])
        for tb in range(4):
            for hf in range(2):
                if early_x and tb == 0 and hf == 0:
                    continue
                DMA("sp", hT3[:, 4 * hf:4 * hf + 4, tb * 512:(tb + 1) * 512],
                    xT_v[:, 4 * hf:4 * hf + 4, HALO + tb * 512: HALO + tb * 512 + 512], (),
                    [("hT", oc, tb) for oc in range(4 * hf, 4 * hf + 4)])

        def mix_src(kc, tb):
            if kc < 4:
                return attnT[:, kc * TOK + tb * 512: kc * TOK + tb * 512 + 512], ("attnT", kc, tb)
            return pooled[:, (kc - 4) * TOK + tb * 512:(kc - 4) * TOK + tb * 512 + 512], ("pooled", kc - 4, tb)

        hn = view(R_C, 8 * TOK, BF16)
        wckv = view(R_D, 8192, BF16)
        wcq = view(R_D + 16 * KB, 4096, BF16)
        wco = view(R_D + 24 * KB, 4096, BF16)
        WLOAD(wckv[:, 0:4096], w_ckv_d[:, 0:4096], [("wckv", 0)])
        WLOAD(wckv[:, 4096:8192], w_ckv_d[:, 4096:8192], [("wckv", 1)])
        WLOAD(wcq, w_cq_d, ["wcq"])
        memf = view(R_E, 8 * 256, F32)
        memn = view(R_E + 8 * KB, 8 * 256, BF16)

        def hn_steps(tb):
            return rms_steps(lambda kc, tb=tb: hT[:, kc * TOK + tb * 512: kc * TOK + tb * 512 + 512],
                             lambda kc, tb=tb: hn[:, kc * TOK + tb * 512: kc * TOK + tb * 512 + 512], C_CROSS,
                             [("hT", oc, tb) for oc in range(8)], lambda kc, tb=tb: ("hn", kc, tb))

        RORDER = [0, 1, 2, 3, 8, 9, 10, 11, 4, 5, 6, 7, 12, 13, 14, 15] + list(range(16, 25))
        def mem_rms_steps():
            th = []
            for kc in range(8):
                def f(kc=kc):
                    ACT(rsq[kc % 4][:, 0:256], memf[:, kc * 256:(kc + 1) * 256], AF.Square, ["memf"], [("rsq", kc % 4)])
                    MM(psb[7][:, 0:256], ones, rsq[kc % 4][:, 0:256], kc == 0, kc == 7, [("rsq", kc % 4), "ones"], [("ps", 7)])
                th.append(f)

            def g_():
                ACT(rsd[:, 0:256], psb[7][:, 0:256], AF.Ln, [("ps", 7), "cols_eps"], ["rsd"], bias=col(C_EPS), scale=1.0 / D)
                ACT(rsd[:, 0:256], rsd[:, 0:256], AF.Exp, ["rsd"], ["rsd"], scale=-0.5)
            th.append(g_)
            for kc in range(8):
                def h_(kc=kc):
                    STT(memn[:, kc * 256:(kc + 1) * 256], memf[:, kc * 256:(kc + 1) * 256], col(C_MEM + kc), rsd[:, 0:256],
                        ALU.mult, ALU.mult, ["memf", "rsd", "cols"], ["memn"])
                th.append(h_)
            return th

        n_ = 0
        hn3_pend = []
        for tb in range(5):
            pend = []
            if tb >= 1:
                st_ = hn_steps(tb - 1)
                pend = [st_[k_] for k_ in RORDER]
            if tb == 1:
                pend = pend + mem_rms_steps()
            if tb == 4:
                hn3_pend = pend
                break
            for oc in range(8):
                bank = n_ % 4
                n_ += 1
                for kc in range(8):
                    src, skey = mix_src(kc, tb)
                    MM(psb[bank][:, :], wout[:, kc * 1024 + oc * 128: kc * 1024 + oc * 128 + 128], src, kc == 0, kc == 7,
                       [("wout", kc // 4), skey], [("ps", bank)])
                hv = hT[:, oc * TOK + tb * 512: oc * TOK + tb * 512 + 512]
                TT("dve", hv, psb[bank][:, :], hv, ALU.add, [("ps", bank), ("hT", oc, tb)], [("hT", oc, tb)])
                lo, hi_ = (len(pend) * oc) // 8, (len(pend) * (oc + 1)) // 8
                for f in pend[lo:hi_]:
                    f()
        if DBG["stop"] == 6:
            P.barrier()
        else:
            P.sync_on(["pe"], ["act", "dve", "pool", "sp"])
        stop_here(6, [(hT, 8192)])
        WLOAD(wco, w_co_d, ["wco"])

        qcT = view(R_F, 4 * TOK, BF16)
        ocT = view(R_W, 4 * TOK, BF16)
        kcT = view(R_E + 12 * KB, 4 * 256, BF16)
        vc = view(R_E + 14 * KB, 2 * 512, BF16)
        sqb = [view(R_G + i * KB, 512, BF16) for i in range(2)]
        sdr = [view(R_G + 2 * KB + i * 2 * KB, 512, F32) for i in range(2)]
        Pc = [view(R_G + 6 * KB + i * KB, 512, BF16) for i in range(6)]
        rdn = [view(R_G + 12 * KB + i * 2 * KB, 512, F32) for i in range(2)]
        for hh in range(4):
            bank = hh % 2
            for kc in range(8):
                MM(psb[bank][:, 0:256], wckv[:, kc * 1024 + hh * 128: kc * 1024 + hh * 128 + 128], memn[:, kc * 256:(kc + 1) * 256],
                   kc == 0, kc == 7, [("wckv", kc // 4), "memn"], [("ps", bank)])
            s = sqb[hh % 2]
            ACT(s[:, 0:256], psb[bank][:, 0:256], AF.Square, [("ps", bank)], [("sq", hh % 2)])
            sb = 2 + hh % 2
            MM(psb[sb][:, 0:256], ones, s[:, 0:256], True, True, [("sq", hh % 2), "ones"], [("ps", sb)])
            sd = sdr[hh % 2]
            ACT(sd[:, 0:256], psb[sb][:, 0:256], AF.Ln, [("ps", sb), "cols_eps"], [("sdr", hh % 2)], bias=col(C_EPS), scale=1.0 / 128)
            ACT(sd[:, 0:256], sd[:, 0:256], AF.Exp, [("sdr", hh % 2)], [("sdr", hh % 2)], scale=-0.5)
            STT(kcT[:, hh * 256:(hh + 1) * 256], psb[bank][:, 0:256], col(C_CK), sd[:, 0:256], ALU.mult, ALU.mult,
                [("ps", bank), ("sdr", hh % 2), "cols"], [("kcT", hh)])
            for _ in range(5):
                if hn3_pend:
                    hn3_pend.pop(0)()
        for mt in range(2):
            bank = 4 + mt
            for kc in range(8):
                MM(psb[bank][:, :], memn[:, kc * 256 + mt * 128: kc * 256 + mt * 128 + 128], wckv[:, kc * 1024 + 512: kc * 1024 + 1024],
                   kc == 0, kc == 7, [("wckv", kc // 4), "memn"], [("ps", bank)])
            CP("act", vc[:, mt * 512:(mt + 1) * 512], psb[bank][:, :], [("ps", bank)], [("vc", mt)])
            for _ in range(5):
                if hn3_pend:
                    hn3_pend.pop(0)()
        while hn3_pend:
            hn3_pend.pop(0)()
        qunits = [(tb, hh) for tb in range(4) for hh in range(4)]

        def q_st0(i):
            tb, hh = qunits[i]
            bank = i % 3
            for kc in range(8):
                MM(psb[bank][:, :], wcq[:, kc * 512 + hh * 128: kc * 512 + hh * 128 + 128],
                   hn[:, kc * TOK + tb * 512: kc * TOK + tb * 512 + 512], kc == 0, kc == 7,
                   ["wcq", ("hn", kc, tb)], [("ps", bank)])

        def q_st1(i):
            tb, hh = qunits[i]
            bank = i % 3
            s_ = sqb[i % 2]
            ACT(s_, psb[bank][:, :], AF.Square, [("ps", bank)], [("sq", i % 2)])
            sb = 3 + i % 2
            MM(psb[sb][:, :], ones, s_, True, True, [("sq", i % 2), "ones"], [("ps", sb)])
            sd = sdr[i % 2]
            ACT(sd, psb[sb][:, :], AF.Ln, [("ps", sb), "cols_eps"], [("sdr", i % 2)], bias=col(C_EPS), scale=1.0 / 128)
            ACT(sd, sd, AF.Exp, [("sdr", i % 2)], [("sdr", i % 2)], scale=-0.5)
            STT(qcT[:, hh * TOK + tb * 512: hh * TOK + tb * 512 + 512], psb[bank][:, :], col(C_CQS), sd, ALU.mult, ALU.mult,
                [("ps", bank), ("sdr", i % 2), "cols_qs"], [("qcT", hh, tb)])

        for i in range(len(qunits) + 1):
            if i < len(qunits):
                q_st0(i)
            if i >= 1:
                q_st1(i - 1)
        citems = [(tb, hh, mt) for tb in range(4) for hh in range(4) for mt in range(2)]

        def c_S(k):
            tb, hh, mt = citems[k]
            sbank = k % 4
            MM(psb[sbank][:, :], kcT[:, hh * 256 + mt * 128: hh * 256 + mt * 128 + 128],
               qcT[:, hh * TOK + tb * 512: hh * TOK + tb * 512 + 512], True, True,
               [("kcT", hh), ("qcT", hh, tb)], [("ps", sbank)])
            ACT(Pc[k % 6], psb[sbank][:, :], AF.Exp, [("ps", sbank)], [("pc", k % 6)])

        def c_PV(k):
            tb, hh, mt = citems[k]
            u_ = k // 2
            nbank = 4 + u_ % 2
            dbank = 6 + u_ % 2
            pc = Pc[k % 6]
            pk = ("pc", k % 6)
            MM(psb[nbank][:, :], vc[:, mt * 512 + hh * 128: mt * 512 + hh * 128 + 128], pc, mt == 0, mt == 1,
               [("vc", mt), pk], [("ps", nbank)])
            MM(psb[dbank][:, :], ones, pc, mt == 0, mt == 1, ["ones", pk], [("ps", dbank)])
            if mt == 1:
                rd = rdn[u_ % 2]
                ACT(rd, psb[dbank][:, :], AF.Ln, [("ps", dbank)], [("rdn", u_ % 2)])
                ACT(rd, rd, AF.Exp, [("rdn", u_ % 2)], [("rdn", u_ % 2)], scale=-1.0)
                TT("dve", ocT[:, hh * TOK + tb * 512: hh * TOK + tb * 512 + 512], psb[nbank][:, :], rd, ALU.mult,
                   [("ps", nbank), ("rdn", u_ % 2)], [("ocT", hh, tb)])

        CL = 3
        for k in range(len(citems) + CL):
            if k < len(citems):
                c_S(k)
            if k - CL >= 0:
                c_PV(k - CL)
        hnh = view(R_C, 8 * 1024, BF16)
        actT = view(R_C + 16 * KB, 22 * 1024, BF16)
        wgu = [view(R_E + i * 8 * KB, 4096, BF16) for i in range(2)]
        wdn = [view(152 * KB + i * 11264, 22 * 256, BF16) for i in range(2)]
        sgb = [view(132 * KB + i * 2 * KB, 512, F32) for i in range(2)]
        ost = [view(176 * KB + i * 2 * KB, 512, F32) for i in range(3)]
        sqb = [view(R_G + i * KB, 512, BF16) for i in range(2)]
        sdr = [view(R_G + 2 * KB + i * 2 * KB, 512, F32) for i in range(2)]
        outT_v = outT.rearrange("(k p) t -> p k t", p=128)
        final_ops = []
        tasks = []
        for half in range(2):
            for fg in range(11):
                tasks.append(("gu", half, fg))
            for og in range(4):
                tasks.append(("dn", half, og))
        cnts = {"gu": 0, "dn": 0}
        tbuf = []
        for (kind, half, idx) in tasks:
            tbuf.append(cnts[kind] % 2)
            cnts[kind] += 1

        def issue_load(k):
            kind, half, idx = tasks[k]
            wi = tbuf[k]
            if kind == "gu":
                WLOAD(wgu[wi], w_gu_d[idx], [("wgu", wi), "memf", "memn"] + [("kcT", h_) for h_ in range(4)] + [("vc", m_) for m_ in range(2)])
            else:
                WLOAD(wdn[wi], w_d_d[idx], [("wdn", wi)])

        issued = set()

        def issue_once(k):
            if k not in issued and k < len(tasks):
                issued.add(k)
                issue_load(k)

        issue_once(0)
        issue_once(1)

        def ffn_rms_steps(tb, t2):
            st_ = rms_steps(lambda kc, tb=tb: hT[:, kc * TOK + tb * 512: kc * TOK + tb * 512 + 512],
                            lambda kc, t2=t2: hnh[:, kc * 1024 + t2 * 512: kc * 1024 + t2 * 512 + 512], C_FFN,
                            [("hT2", oc, tb) for oc in range(8)],
                            lambda kc, t2=t2: [("hnh", kc, t2)] + [("hn", kc_, tb_) for kc_ in range(8) for tb_ in range(4)],
                            bank=7)
            return [st_[k_] for k_ in RORDER]

        n_ = 0
        for tb in range(4):
            pend = ffn_rms_steps(tb - 1, tb - 1) if tb in (1, 2) else []
            for oc in range(8):
                bank = n_ % 4
                n_ += 1
                for kc in range(4):
                    MM(psb[bank][:, :], wco[:, kc * 1024 + oc * 128: kc * 1024 + oc * 128 + 128],
                       ocT[:, kc * TOK + tb * 512: kc * TOK + tb * 512 + 512], kc == 0, kc == 3,
                       ["wco", ("ocT", kc, tb)], [("ps", bank)])
                hv = hT[:, oc * TOK + tb * 512: oc * TOK + tb * 512 + 512]
                TT("dve", hv, psb[bank][:, :], hv, ALU.add, [("ps", bank), ("hT2", oc, tb)], [("hT2", oc, tb)])
                lo, hi_ = (len(pend) * oc) // 8, (len(pend) * (oc + 1)) // 8
                for f in pend[lo:hi_]:
                    f()
        if DBG["stop"] == 7:
            P.barrier()
        else:
            P.sync_on(["pe", "act", "dve", "pool", "sp"], ["act", "dve", "pool", "sp"])
        stop_here(7, [(hT, 8192)])

        st = {"n": 0, "on": 0}
        ffn_pend = []

        def do_gu(half, fg, wi):
            for fl in range(2):
                f = fg * 2 + fl
                for t2 in range(2):
                    n_ = st["n"]
                    gb = (2 * n_) % 4
                    ub = (2 * n_ + 1) % 4
                    for kc in range(8):
                        MM(psb[gb][:, :], wgu[wi][:, kc * 512 + fl * 128: kc * 512 + fl * 128 + 128],
                           hnh[:, kc * 1024 + t2 * 512: kc * 1024 + t2 * 512 + 512], kc == 0, kc == 7,
                           [("wgu", wi), ("hnh", kc, t2)], [("ps", gb)])
                    for kc in range(8):
                        MM(psb[ub][:, :], wgu[wi][:, kc * 512 + 256 + fl * 128: kc * 512 + 256 + fl * 128 + 128],
                           hnh[:, kc * 1024 + t2 * 512: kc * 1024 + t2 * 512 + 512], kc == 0, kc == 7,
                           [("wgu", wi), ("hnh", kc, t2)], [("ps", ub)])
                    sg = sgb[n_ % 2]
                    ACT(sg, psb[gb][:, :], AF.Silu, [("ps", gb)], [("sg", n_ % 2)])
                    TT("dve", actT[:, f * 1024 + t2 * 512: f * 1024 + t2 * 512 + 512], sg, psb[ub][:, :], ALU.mult,
                       [("sg", n_ % 2), ("ps", ub)], [("actT", f, t2)])
                    st["n"] += 1

        def do_dn(half, og, wi):
            for ol in range(2):
                oc = og * 2 + ol
                for t2 in range(2):
                    tb = half * 2 + t2
                    on = st["on"]
                    bank = 4 + on % 2
                    for f in range(22):
                        MM(psb[bank][:, :], wdn[wi][:, f * 256 + ol * 128: f * 256 + ol * 128 + 128],
                           actT[:, f * 1024 + t2 * 512: f * 1024 + t2 * 512 + 512], f == 0, f == 21,
                           [("wdn", wi), ("actT", f, t2)], [("ps", bank)])
                    o_ = ost[on % 3]
                    TT("dve", o_, psb[bank][:, :], hT[:, oc * TOK + tb * 512: oc * TOK + tb * 512 + 512], ALU.add,
                       [("ps", bank)], [("ost", on % 3)])
                    final_ops.append(DMA("sp", outT_v[:, oc, tb * 512:(tb + 1) * 512], o_, [("ost", on % 3)], ()))
                    st["on"] += 1
                    for _ in range(5):
                        if ffn_pend:
                            ffn_pend.pop(0)()

        for k, (kind, half, idx) in enumerate(tasks):
            issue_once(k)
            issue_once(k + 1)
            if kind == "gu":
                do_gu(half, idx, tbuf[k])
            else:
                if half == 0 and idx == 1:
                    for t2 in range(2):
                        ffn_pend.extend(ffn_rms_steps(2 + t2, t2))
                do_dn(half, idx, tbuf[k])
                if half == 0 and idx == 3:
                    while ffn_pend:
                        ffn_pend.pop(0)()
        P.finalize_and_emit(final_ops)
        DBG["stats"] = P.stats
    return nc


def _tile_w(w, ncols_group):
    K, N = w.shape
    kc = K // 128
    ng = N // ncols_group
    t = w.reshape(kc, 128, ng, ncols_group).transpose(2, 1, 0, 3).reshape(ng, 128, kc * ncols_group)
    return np.ascontiguousarray(t)


_NC_CACHE = {}


def kernel(x, mem, positions, mix_norm_g, w_in, q_norm_g, k_norm_g, pool_w, pool_scale, w_out,
           cross_norm_g, mem_norm_g, w_cq, w_ckv, cq_norm_g, ck_norm_g, w_co, ffn_norm_g, w_gate_up, w_down):
    f32 = np.float32
    x = np.asarray(x, f32)
    mem = np.asarray(mem, f32)
    positions = np.asarray(positions, np.int32)
    B, S, _ = x.shape
    cpb = S // TOK

    w_in0 = np.asarray(w_in, f32)[0]
    order = [(2560, 3072), (1536, 2048), (2048, 2560), (0, 512), (512, 1024), (1024, 1536)]
    w_in_t = np.stack([_tile_w(w_in0[:, a:b], 512)[0] for a, b in order])
    w_out_t = _tile_w(np.asarray(w_out, f32)[0], 1024)[0]
    w_cq_t = _tile_w(np.asarray(w_cq, f32)[0], 512)[0]
    w_ckv_t = _tile_w(np.asarray(w_ckv, f32)[0], 1024)[0]
    w_co_t = _tile_w(np.asarray(w_co, f32)[0], 1024)[0]
    wgu = np.asarray(w_gate_up, f32)[0]
    gate_t = _tile_w(wgu[:, :DFF], 256)
    up_t = _tile_w(wgu[:, DFF:], 256)
    w_gu_t = np.ascontiguousarray(
        np.concatenate([gate_t.reshape(11, 128, 8, 256), up_t.reshape(11, 128, 8, 256)], axis=3).reshape(11, 128, 4096))
    w_d_t = _tile_w(np.asarray(w_down, f32)[0], 256)
    pool_w_t = np.ascontiguousarray(np.asarray(pool_w, f32)[0].transpose(1, 0, 2).reshape(128, 512))

    cols = np.zeros((128, NCOL), f32)

    def colmat(v):
        v = np.asarray(v, f32).reshape(-1)
        return v.reshape(-1, 128).T

    cols[:, C_MIX:C_MIX + 8] = colmat(mix_norm_g)
    cols[:, C_CROSS:C_CROSS + 8] = colmat(cross_norm_g)
    cols[:, C_MEM:C_MEM + 8] = colmat(mem_norm_g)
    cols[:, C_FFN:C_FFN + 8] = colmat(ffn_norm_g)
    cols[:, C_Q] = np.asarray(q_norm_g, f32).reshape(-1)
    cols[:, C_K] = np.asarray(k_norm_g, f32).reshape(-1)
    cols[:, C_CQ] = np.asarray(cq_norm_g, f32).reshape(-1)
    cols[:, C_CK] = np.asarray(ck_norm_g, f32).reshape(-1)
    cols[:, C_PS:C_PS + 4] = colmat(pool_scale)
    inv_freq = (500000.0 ** (-np.arange(0, 32, 2, dtype=np.float32) / np.float32(32))).astype(f32)
    cols[:, C_INVF] = np.tile(inv_freq, 8)

    kk = np.arange(128)[:, None]
    qq = np.arange(128)[None, :]
    cbs = np.zeros((128, NCB), f32)
    cbs[:, CB_ID:CB_ID + 128] = np.eye(128, dtype=f32)
    rm = np.zeros((128, 128), f32)
    for m in range(16):
        rm[m + 16, m] = -1.0
        rm[m, m + 16] = 1.0
    cbs[:, CB_RM:CB_RM + 128] = rm
    prev = (kk >= qq).astype(f32)
    cur = (kk <= qq).astype(f32)
    cbs[:, CB_MS:CB_MS + 256] = np.concatenate([prev, cur], axis=1)
    cb_first = cbs.copy()
    cb_other = cbs.copy()
    cb_first[:, CB_MF:CB_MF + 256] = np.concatenate([np.zeros_like(prev), cur], axis=1)
    cb_other[:, CB_MF:CB_MF + 256] = np.concatenate([prev, cur], axis=1)
    cnt_first = np.zeros((128, 64), f32)
    cnt_other = np.zeros((128, 64), f32)
    for g, w in enumerate((2, 4, 8, 16)):
        t = np.arange(16)
        cnt_first[:, g * 16:(g + 1) * 16] = (1.0 / np.minimum(t + 1, w)).astype(f32)[None, :]
        cnt_other[:, g * 16:(g + 1) * 16] = np.float32(1.0 / w)

    in_maps = []
    for c in range(N_CORES):
        b = c // cpb
        s0 = (c % cpb) * TOK
        xt = np.zeros((D, HALO + TOK), f32)
        pp = np.zeros((1, HALO + TOK), np.int32)
        if s0 > 0:
            xt[:, :HALO] = x[b, s0 - HALO:s0, :].T
            pp[0, :HALO] = positions[b, s0 - HALO:s0]
        xt[:, HALO:] = x[b, s0:s0 + TOK, :].T
        pp[0, HALO:] = positions[b, s0:s0 + TOK]
        first = (s0 == 0)
        in_maps.append({
            "xT": xt, "posb": np.ascontiguousarray(np.repeat(pp.reshape(8, 1, 512), 16, axis=1).reshape(128, 512)), "memT": np.ascontiguousarray(mem[b].T), "cols": cols,
            "cb": cb_first if first else cb_other, "cntinv": cnt_first if first else cnt_other,
            "w_in_t": w_in_t, "w_out_t": w_out_t, "w_cq_t": w_cq_t, "w_ckv_t": w_ckv_t, "w_co_t": w_co_t,
            "w_gu_t": w_gu_t, "w_d_t": w_d_t, "pool_w_t": pool_w_t,
        })

    if "nc" not in _NC_CACHE:
        _NC_CACHE["nc"] = build_program()
    nc = _NC_CACHE["nc"]
    res = run_bass_kernel_spmd(nc, in_maps, core_ids=list(range(N_CORES)))
    DBG["res"] = res
    out = np.empty((B, S, D), f32)
    for c in range(N_CORES):
        b = c // cpb
        s0 = (c % cpb) * TOK
        out[b, s0:s0 + TOK, :] = res.results[c]["outT"].T
    return out
```

```python
import contextlib
import numpy as np
import concourse.bass as bass
import concourse.mybir as mybir
from concourse.bass_utils import run_bass_kernel_spmd

F32 = mybir.dt.float32
BF16 = mybir.dt.bfloat16
I32 = mybir.dt.int32
AF = mybir.ActivationFunctionType
ALU = mybir.AluOpType

EPOCH = 12000
NDMASEM = {"sp": 20, "pool": 6, "act": 4, "pe": 1, "dve": 1}

N_CORES = 8
TOK = 2048
HALO = 2048
D = 1024
DFF = 2816
PI = float(np.pi)


class Op:
    __slots__ = ("eng", "emit", "deps", "needs_inc", "is_dma", "sem", "val", "prev_val")

    def __init__(self, eng, emit, is_dma):
        self.eng = eng
        self.emit = emit
        self.deps = []
        self.needs_inc = False
        self.is_dma = is_dma
        self.sem = None
        self.val = 0
        self.prev_val = 0


class Prog:
    ENGS = ("pe", "act", "dve", "pool", "sp")

    def __init__(self, nc):
        self.nc = nc
        self.streams = {e: [] for e in self.ENGS}
        self.last_w = {}
        self.readers = {}

    def _add(self, eng, emit, reads, writes, is_dma):
        o = Op(eng, emit, is_dma)
        psr = [r for r in reads if isinstance(r, tuple) and r and r[0] in ("ps", "pss")]
        if psr:
            writes = list(writes) + psr
        deps = {}
        for r in reads:
            w = self.last_w.get(r)
            if w is not None:
                deps[id(w)] = w
        for k in writes:
            w = self.last_w.get(k)
            if w is not None:
                deps[id(w)] = w
            for rd in self.readers.get(k, ()):
                deps[id(rd)] = rd
        for r in reads:
            lst = self.readers.setdefault(r, [])
            if lst and lst[-1].eng == eng and not lst[-1].is_dma and not is_dma:
                lst[-1] = o
            else:
                lst.append(o)
        for k in writes:
            self.last_w[k] = o
            self.readers[k] = []
        for d in deps.values():
            if d is o:
                continue
            if d.eng == "pe" and eng == "pe" and not d.is_dma and not is_dma:
                continue
            o.deps.append(d)
            d.needs_inc = True
        self.streams[eng].append(o)
        return o

    def op(self, eng, emit, reads=(), writes=()):
        return self._add(eng, emit, reads, writes, False)

    def dma(self, eng, emit, reads=(), writes=()):
        o = self._add(eng, emit, reads, writes, True)
        o.needs_inc = True
        return o

    def barrier(self):
        lasts = []
        for e in self.ENGS:
            st = self.streams[e]
            for o in reversed(st):
                if not o.is_dma and o.emit is not None:
                    lasts.append(o)
                    break
            cnt = 0
            for o in reversed(st):
                if o.is_dma:
                    lasts.append(o)
                    cnt += 1
                    if cnt >= NDMASEM[e]:
                        break
        for e in self.ENGS:
            o = Op(e, None, False)
            for d in lasts:
                if d.eng == "pe" and e == "pe" and not d.is_dma:
                    continue
                o.deps.append(d)
                d.needs_inc = True
            self.streams[e].append(o)

    def sync_on(self, src_engs, dst_engs):
        lasts = []
        for e in src_engs:
            st = self.streams[e]
            for o in reversed(st):
                if not o.is_dma and o.emit is not None:
                    lasts.append(o)
                    break
            cnt = 0
            for o in reversed(st):
                if o.is_dma:
                    lasts.append(o)
                    cnt += 1
                    if cnt >= NDMASEM[e]:
                        break
        for e in dst_engs:
            o = Op(e, None, False)
            for d in lasts:
                if d.eng == "pe" and e == "pe" and not d.is_dma:
                    continue
                o.deps.append(d)
                d.needs_inc = True
            self.streams[e].append(o)

    def finalize_and_emit(self, final_wait_ops):
        nc = self.nc
        n_inc = {e: 0 for e in self.ENGS}
        for e in self.ENGS:
            for o in self.streams[e]:
                if not o.is_dma and o.needs_inc:
                    n_inc[e] += 1
        with contextlib.ExitStack() as es:
            eng_sems = {}
            for e in self.ENGS:
                ne = max(1, -(-n_inc[e] // EPOCH))
                eng_sems[e] = [es.enter_context(nc.semaphore(f"s_{e}_{i}")) for i in range(ne)]
            dma_sems = {}
            for e in self.ENGS:
                dma_sems[e] = [es.enter_context(nc.semaphore(f"d_{e}_{i}")) for i in range(NDMASEM[e])]
            for e in self.ENGS:
                c = 0
                dcount = [0] * NDMASEM[e]
                nd = 0
                for o in self.streams[e]:
                    if o.is_dma:
                        si = nd % NDMASEM[e]
                        nd += 1
                        o.sem = dma_sems[e][si]
                        o.prev_val = dcount[si]
                        dcount[si] += 16
                        o.val = dcount[si]
                    elif o.needs_inc:
                        o.sem = eng_sems[e][c // EPOCH]
                        o.val = c % EPOCH + 1
                        c += 1
            stats = {}

            def run_stream(e, eh):
                known = {}
                nw = 0
                for o in self.streams[e]:
                    waits = {}
                    for d in o.deps:
                        key = id(d.sem)
                        if known.get(key, 0) >= d.val:
                            continue
                        if key not in waits or waits[key][1] < d.val:
                            waits[key] = (d.sem, d.val)
                    if o.is_dma and o.prev_val > 0:
                        key = id(o.sem)
                        if known.get(key, 0) < o.prev_val:
                            if key not in waits or waits[key][1] < o.prev_val:
                                waits[key] = (o.sem, o.prev_val)
                    for key, (s, v) in waits.items():
                        eh.wait_ge(s, v)
                        known[key] = v
                        nw += 1
                    if o.emit is None:
                        continue
                    ins = o.emit(eh)
                    if o.is_dma:
                        ins.then_inc(o.sem, 16)
                    elif o.needs_inc:
                        ins.then_inc(o.sem, 1)
                if e == "sp":
                    for o in final_wait_ops:
                        if known.get(id(o.sem), 0) < o.val:
                            eh.wait_ge(o.sem, o.val)
                            known[id(o.sem)] = o.val
                stats[e] = (len(self.streams[e]), nw)

            with nc.Block() as block:
                @block.tensor
                def _(eh):
                    run_stream("pe", eh)

                @block.scalar
                def _(eh):
                    run_stream("act", eh)

                @block.vector
                def _(eh):
                    run_stream("dve", eh)

                @block.gpsimd
                def _(eh):
                    run_stream("pool", eh)

                @block.sync
                def _(eh):
                    run_stream("sp", eh)
            self.stats = stats


ARENA_BYTES = 207 * 1024
KB = 1024
C_MIX, C_CROSS, C_MEM, C_FFN = 0, 8, 16, 24
C_Q, C_K, C_CQ, C_CK, C_PS, C_INVF = 32, 33, 34, 35, 36, 40
C_QS, C_CQS, C_EPS = 41, 42, 43
NCOL = 48
CB_ID, CB_RM, CB_MS, CB_MF, NCB = 0, 128, 256, 512, 768

DILS = (1, 4, 16)
DBG = {"stop": None}


class _Stop(Exception):
    pass


def build_program():
    try:
        return _build_program_inner()
    except _Stop as ex:
        return ex.nc


def _build_program_inner():
    nc = bass.Bass("TRN2", target_bir_lowering=False)
    dt = lambda name, shape, dtp, kind: nc.dram_tensor(name, shape, dtp, kind=kind).ap()
    xT = dt("xT", [D, HALO + TOK], F32, "ExternalInput")
    posb_d = dt("posb", [128, 512], I32, "ExternalInput")
    memT = dt("memT", [D, 256], F32, "ExternalInput")
    cols_d = dt("cols", [128, NCOL], F32, "ExternalInput")
    cb_d = dt("cb", [128, NCB], F32, "ExternalInput")
    cnt_d = dt("cntinv", [128, 64], F32, "ExternalInput")
    w_in_d = dt("w_in_t", [6, 128, 4096], F32, "ExternalInput")
    w_out_d = dt("w_out_t", [128, 8192], F32, "ExternalInput")
    w_cq_d = dt("w_cq_t", [128, 4096], F32, "ExternalInput")
    w_ckv_d = dt("w_ckv_t", [128, 8192], F32, "ExternalInput")
    w_co_d = dt("w_co_t", [128, 4096], F32, "ExternalInput")
    w_gu_d = dt("w_gu_t", [11, 128, 4096], F32, "ExternalInput")
    w_d_d = dt("w_d_t", [4, 128, 22 * 256], F32, "ExternalInput")
    pw_d = dt("pool_w_t", [128, 512], F32, "ExternalInput")
    outT = dt("outT", [D, TOK], F32, "ExternalOutput")
    dbg_d = dt("dbg", [128, 16384], F32, "ExternalOutput") if DBG["stop"] is not None else None

    es = contextlib.ExitStack()
    with es:
        arena = es.enter_context(nc.sbuf_tensor("arena", [128, ARENA_BYTES // 4], F32))
        psb = [es.enter_context(nc.psum_tensor(f"ps{i}", [128, 512], F32)) for i in range(8)]
        P = Prog(nc)

        def view(off, nelem, dtp):
            sz = 2 if dtp == BF16 else 4
            assert off % 32 == 0, off
            nb = nelem * sz
            assert nb % 4 == 0 and off + nb <= ARENA_BYTES, (off, nb)
            v = arena[:, off // 4:(off + nb) // 4]
            if dtp != F32:
                v = v.bitcast(dtp)
            return v

        def ap2(v, p0, npart, col0, dims):
            pstride = v.ap[0][0]
            return bass.AP(v.tensor, v.offset + p0 * pstride + col0, [[pstride, npart]] + [list(d) for d in dims])

        def MM(out, lhsT, rhs, start, stop, r, w):
            P.op("pe", lambda e: e.matmul(out, lhsT=lhsT, rhs=rhs, start=start, stop=stop), r, w)

        def TR(out, in_, ident, r, w):
            P.op("pe", lambda e: e.transpose(out=out, in_=in_, identity=ident), r, w)

        def ACT(out, in_, func, r, w, bias=None, scale=None):
            kw = {}
            if bias is not None:
                kw["bias"] = bias
            if scale is not None:
                kw["scale"] = scale
            P.op("act", lambda e: e.activation(out=out, in_=in_, func=func, **kw), r, w)

        def TT(eng, out, in0, in1, op, r, w):
            P.op(eng, lambda e: e.tensor_tensor(out=out, in0=in0, in1=in1, op=op), r, w)

        def TS(eng, out, in0, s1, s2, op0, op1, r, w):
            if op1 is None:
                P.op(eng, lambda e: e.tensor_scalar(out=out, in0=in0, scalar1=s1, scalar2=None, op0=op0), r, w)
            else:
                P.op(eng, lambda e: e.tensor_scalar(out=out, in0=in0, scalar1=s1, scalar2=s2, op0=op0, op1=op1), r, w)

        def STT(out, in0, scalar, in1, op0, op1, r, w):
            P.op("dve", lambda e: e.scalar_tensor_tensor(out=out, in0=in0, scalar=scalar, in1=in1, op0=op0, op1=op1), r, w)

        def CP(eng, out, in_, r, w):
            if eng == "act":
                ACT(out, in_, AF.Copy, r, w)
            else:
                P.op(eng, lambda e: e.tensor_copy(out=out, in_=in_), r, w)

        def RCP(out, in_, r, w):
            P.op("dve", lambda e: e.reciprocal(out=out, in_=in_), r, w)

        def MEMSET(eng, out, val, w):
            P.op(eng, lambda e: e.memset(out, val), (), w)

        def DMA(q, out, in_, r, w, **kw):
            return P.dma(q, lambda e: e.dma_start(out=out, in_=in_, **kw), r, w)

        def WLOAD(out, in_, w):
            return P.dma("pool", lambda e: e.dma_start(out=out, in_=in_, max_dma_last_dim=4096), (), w)

        def stop_here(k, dumps):
            if DBG["stop"] != k:
                return
            P.barrier()
            ops = []
            c0 = 0
            for (v, n) in dumps:
                ops.append(P.dma("pool", lambda e, v=v, c0=c0, n=n: e.dma_start(out=dbg_d[:, c0:c0 + n], in_=v[:, 0:n], max_dma_last_dim=4096), (), ()))
                c0 += n
            P.finalize_and_emit(ops)
            ex = _Stop()
            ex.nc = nc
            raise ex

        cols = view(0, NCOL, F32)
        cb = view(256, NCB, BF16)
        ones = view(1792, 128, BF16)
        cnt = view(2048, 64, F32)
        poolw = view(2304, 512, BF16)
        ident = cb[:, CB_ID:CB_ID + 128]
        Rm = cb[:, CB_RM:CB_RM + 128]
        maskS = cb[:, CB_MS:CB_MS + 256]
        maskF = cb[:, CB_MF:CB_MF + 256]
        sin_all = view(4 * KB, 512, F32)
        cos_all = view(6 * KB, 512, F32)

        def col(i, p0=0, n=128):
            return cols[p0:p0 + n, i:i + 1]

        R_X, R_Y, R_C, R_D, R_E, R_F, R_W, R_G = 8 * KB, 40 * KB, 72 * KB, 104 * KB, 136 * KB, 152 * KB, 168 * KB, 184 * KB
        xn_halo = view(R_X, 8 * 2048, BF16)
        xn_own = view(R_Y, 8 * 2048, BF16)
        KT = view(R_C, 4 * 4096, BF16)
        VT = view(R_D, 4 * 4096, BF16)
        QT_a = view(R_X, 8 * 2048, BF16)
        QT_b = view(R_E, 4 * 2048, BF16)
        pooled = view(R_F, 4 * 2048, BF16)
        wbuf = [view(R_W, 4096, BF16), view(R_W + 8 * KB, 4096, BF16)]
        attnT = view(R_G, 4 * 2048, BF16)

        def xn_blk(b, kc):
            v = xn_halo if b < 4 else xn_own
            bb = b % 4
            return v[:, kc * 2048 + bb * 512: kc * 2048 + bb * 512 + 512]

        def qt_head(qh):
            if qh < 8:
                return QT_a[:, qh * 2048:(qh + 1) * 2048]
            return QT_b[:, (qh - 8) * 2048:(qh - 7) * 2048]

        DMA("sp", cols[:, 0:41], cols_d[:, 0:41], (), ["cols"])
        DMA("sp", cnt, cnt_d, (), ["cnt"])
        WLOAD(cb, cb_d, ["cb"])
        WLOAD(poolw, pw_d, ["poolw"])
        MEMSET("pool", ones, 1.0, ["ones"])
        MEMSET("pool", cols[:, C_EPS:C_EPS + 1], 1e-6, ["cols_eps"])
        sc = float(128.0 ** -0.5)
        TS("dve", cols[:, C_QS:C_QS + 1], cols[:, C_Q:C_Q + 1], sc, None, ALU.mult, None, ["cols"], ["cols_qs"])
        TS("dve", cols[:, C_CQS:C_CQS + 1], cols[:, C_CQ:C_CQ + 1], sc, None, ALU.mult, None, ["cols"], ["cols_qs"])
        CONSTS = ["cols", "cols_eps", "cols_qs", "cb", "ones", "cnt", "poolw"]

        posb = view(R_G + 2 * KB, 512, I32)
        DMA("sp", posb, posb_d, (), ["posb"])
        WLOAD(wbuf[1], w_in_d[1], [("wbuf", 1)])
        P.dma("pool", lambda e: e.dma_start(out=wbuf[0], in_=w_in_d[2], max_dma_last_dim=4096), [("wbuf", 1)], [("wbuf", 0)])
        xst = [view(R_E, 4096, F32), view(R_F, 4096, F32)]
        xT_v = xT.rearrange("(k p) t -> p k t", p=128)
        DMA("sp", xst[0].rearrange("p (k t) -> p k t", k=8), xT_v[:, :, 0:512], (), [("xst", 0)])
        DMA("sp", xst[1].rearrange("p (k t) -> p k t", k=8), xT_v[:, :, 512:1024], [("xst", 0)], [("xst", 1)])
        def emit_trig():
          ang = view(R_G + 4 * KB, 512, F32)
          kf = view(R_G + 6 * KB, 512, F32)
          ki = view(R_G + 8 * KB, 512, I32)
          ang2 = view(R_G + 10 * KB, 512, F32)
          TS("dve", ang, posb, col(C_INVF), None, ALU.mult, None, ["posb", "cols"], ["ang"])
          C1 = 6.28125
          C2 = float(2 * np.pi - 6.28125)
          TS("dve", kf, ang, float(1.0 / (2 * np.pi)), None, ALU.mult, None, ["ang"], ["kf"])
          CP("dve", ki, kf, ["kf"], ["ki"])
          CP("dve", kf, ki, ["ki"], ["kf"])
          STT(ang, kf, -C1, ang, ALU.mult, ALU.add, ["kf", "ang"], ["ang"])
          STT(ang, kf, -C2, ang, ALU.mult, ALU.add, ["kf", "ang"], ["ang"])
          TS("dve", kf, ang, PI, -2 * PI, ALU.is_gt, ALU.mult, ["ang"], ["kf"])
          TT("dve", ang, ang, kf, ALU.add, ["ang", "kf"], ["ang"])
          TS("dve", kf, ang, -PI, 2 * PI, ALU.is_lt, ALU.mult, ["ang"], ["kf"])
          TT("dve", ang, ang, kf, ALU.add, ["ang", "kf"], ["ang"])
          ACT(sin_all, ang, AF.Sin, ["ang"], ["sin_all"])
          TS("dve", ang2, ang, PI / 2, None, ALU.add, None, ["ang"], ["ang2"])
          TS("dve", kf, ang2, PI, -2 * PI, ALU.is_gt, ALU.mult, ["ang2"], ["kf"])
          TT("dve", ang2, ang2, kf, ALU.add, ["ang2", "kf"], ["ang2"])
          ACT(cos_all, ang2, AF.Sin, ["ang2", "ang", "kf", "ki", "posb"],
              ["cos_all", ("sdr", 0), ("sdr", 1), ("t1", 0), ("t1", 1), "t2"])

        SCR = R_G
        sqb = [view(SCR + i * KB, 512, BF16) for i in range(2)]
        sdr = [view(SCR + 2 * KB + i * 2 * KB, 512, F32) for i in range(2)]
        t1b = [view(SCR + 6 * KB + i * 2 * KB, 512, F32) for i in range(2)]
        t2b = [view(SCR + 10 * KB, 512, F32)]
        CSb = [(view(SCR + 12 * KB, 512, F32), view(SCR + 14 * KB, 512, F32)),
               (view(SCR + 16 * KB, 512, F32), view(SCR + 18 * KB, 512, F32))]
        rsq = [view(SCR + 16 * KB + i * KB, 512, BF16) for i in range(4)]
        rsd = view(SCR + 20 * KB, 512, F32)
        RMS_BANK = 7

        def rms_steps(src_kc, dst_kc, gcol0, srckeys, tagdst, bank=RMS_BANK, sq_eng="act"):
            steps = []
            for kc in range(8):
                def f(kc=kc):
                    if sq_eng == "act":
                        ACT(rsq[kc % 4], src_kc(kc), AF.Square, srckeys, [("rsq", kc % 4)])
                    else:
                        TT(sq_eng, rsq[kc % 4], src_kc(kc), src_kc(kc), ALU.mult, srckeys, [("rsq", kc % 4)])
                steps.append(f)
            for kc in range(8):
                def f2(kc=kc):
                    MM(psb[bank][:, :], ones, rsq[kc % 4], kc == 0, kc == 7, [("rsq", kc % 4), "ones"], [("ps", bank)])
                steps.append(f2)

            def g_():
                ACT(rsd, psb[bank][:, :], AF.Ln, [("ps", bank), "cols_eps"], ["rsd"], bias=col(C_EPS), scale=1.0 / D)
                ACT(rsd, rsd, AF.Exp, ["rsd"], ["rsd"], scale=-0.5)
            steps.append(g_)
            for kc in range(8):
                def h_(kc=kc):
                    td = tagdst(kc)
                    STT(dst_kc(kc), src_kc(kc), col(gcol0 + kc), rsd, ALU.mult, ALU.mult,
                        list(srckeys) + ["rsd", "cols"], td if isinstance(td, list) else [td])
                steps.append(h_)
            return steps

        def rms_block(src_kc, dst_kc, gcol0, srckeys, tagdst, uid=0, ps_bank=RMS_BANK):
            st_ = rms_steps(src_kc, dst_kc, gcol0, srckeys, tagdst, bank=ps_bank)
            order = [0, 1, 2, 3, 8, 9, 10, 11, 4, 5, 6, 7, 12, 13, 14, 15] + list(range(16, 25))
            for k_ in order:
                st_[k_]()

        cs_state = {"n": 0, "nbuf": 1}

        def load_cs(b):
            i = cs_state["n"] % cs_state["nbuf"]
            cs_state["n"] += 1
            Cb, Sb = CSb[i]
            for half in range(2):
                DMA("sp", Cb[16 * half:16 * half + 16, :], cos_all[16 * b:16 * b + 16, :], ["cos_all"], [("cs", i)])
                DMA("sp", Sb[16 * half:16 * half + 16, :], sin_all[16 * b:16 * b + 16, :], ["sin_all"], [("cs", i)])
            return i

        def x_rms_steps(b):
            xs = xst[b % 2]
            return rms_steps(lambda kc, xs=xs: xs[:, kc * 512:(kc + 1) * 512], lambda kc, b=b: xn_blk(b, kc), C_MIX,
                             [("xst", b % 2)], lambda kc, b=b: ("xn", b, kc), sq_eng="act")

        def run_units(units, wsel, dest_fn, gcol_fn, kind_fn, hooks, two_rq=False, hooks_post=None):
            cs_of = {}

            def st0(u, n):
                b, h, tag = u
                if kind_fn(tag) == "qk" and (b, tag) not in cs_of:
                    cs_of[(b, tag)] = load_cs(b)
                wb, wkey = wsel(tag)
                bank = n % 4
                for kc in range(8):
                    MM(psb[bank][:, :], wb[:, kc * 512 + h * 128: kc * 512 + h * 128 + 128], xn_blk(b, kc),
                       kc == 0, kc == 7, [wkey, ("xn", b, kc)], [("ps", bank)])

            def st1a(u, n):
                b, h, tag = u
                bank = n % 4
                dest, dkey = dest_fn(b, h, tag)
                if kind_fn(tag) == "v":
                    CP("act" if n % 2 == 0 else "dve", dest, psb[bank][:, :], [("ps", bank)], [dkey])
                    return
                ACT(sqb[n % 2], psb[bank][:, :], AF.Square, [("ps", bank)], [("sq", n % 2)])

            def st1(u, n):
                b, h, tag = u
                bank = n % 4
                dest, dkey = dest_fn(b, h, tag)
                if kind_fn(tag) == "v":
                    return
                s_ = sqb[n % 2]
                sb = 4 + n % 2
                MM(psb[sb][:, :], ones, s_, True, True, [("sq", n % 2), "ones"], [("ps", sb)])
                sd = sdr[n % 2]
                ACT(sd, psb[sb][:, :], AF.Ln, [("ps", sb), "cols_eps"], [("sdr", n % 2)], bias=col(C_EPS), scale=1.0 / 128)
                ACT(sd, sd, AF.Exp, [("sdr", n % 2)], [("sdr", n % 2)], scale=-0.5)
                STT(dest, psb[bank][:, :], col(gcol_fn(tag)), sd, ALU.mult, ALU.mult,
                    [("ps", bank), ("sdr", n % 2), "cols", "cols_qs"], [dkey])

            def st2a(u, n):
                b, h, tag = u
                if kind_fn(tag) == "v":
                    return
                csi = cs_of[(b, tag)]
                dest, dkey = dest_fn(b, h, tag)
                Cb, Sb = CSb[csi]
                rb = 6 + (n % 2 if two_rq else 0)
                MM(psb[rb][:, :], Rm, dest, True, True, [dkey, "cb"], [("ps", rb)])
                t1 = t1b[n % 2]
                t2 = t2b[0]
                TT("dve", t1[0:32, :], psb[rb][0:32, :], Sb[0:32, :], ALU.mult, [("ps", rb), ("cs", csi)], [("t1", n % 2)])
                TT("pool", t2[0:32, :], dest[0:32, :], Cb[0:32, :], ALU.mult, [dkey, ("cs", csi)], ["t2"])

            def st2b(u, n):
                b, h, tag = u
                if kind_fn(tag) == "v":
                    return
                dest, dkey = dest_fn(b, h, tag)
                TT("dve", dest[0:32, :], t1b[n % 2][0:32, :], t2b[0][0:32, :], ALU.add, [("t1", n % 2), "t2"], [dkey])

            n = len(units)
            for i in range(n + 4):
                if i < n:
                    st0(units[i], i)
                    for f in hooks.get(i, ()):
                        f()
                if 0 <= i - 1 < n:
                    st1a(units[i - 1], i - 1)
                if 0 <= i - 4 < n:
                    st2a(units[i - 4], i - 4)
                if 0 <= i - 2 < n:
                    st1(units[i - 2], i - 2)
                if 0 <= i - 4 < n:
                    st2b(units[i - 4], i - 4)
                if hooks_post is not None:
                    for f in hooks_post.get(i, ()):
                        f()

        wu = view(R_F, 4096, BF16)
        st_ = x_rms_steps(0)
        for k_ in [0, 1, 2, 3, 8, 9, 10, 11, 4, 5, 6, 7, 12, 13, 14, 15] + list(range(16, 25)):
            st_[k_]()
        emit_trig()
        stop_here(0, [(sin_all, 512), (cos_all, 512)])
        units = []
        hooks = {}
        hooks_post = {}
        for b in range(8):
            base = len(units)
            for h in range(4):
                units.append((b, h, "k"))
            for h in range(4):
                units.append((b, h, "v"))
            if b + 1 < 8:
                nb_ = b + 1
                steps = x_rms_steps(nb_)
                per = [[] for _ in range(8)]
                k0 = 0
                n2 = b + 2
                if n2 < 8:
                    per[0].append(lambda n2=n2: DMA("sp", xst[n2 % 2].rearrange("p (k t) -> p k t", k=8),
                                                    xT_v[:, :, 512 * n2:512 * n2 + 512], (), [("xst", n2 % 2)]))
                sq = steps[k0:k0 + 8]; mmr = steps[k0 + 8:k0 + 16]; ln = steps[k0 + 16]; st = steps[k0 + 17:k0 + 25]
                post = [[] for _ in range(8)]
                for j in range(8):
                    per[j // 2].append(sq[j])
                    post[j // 2 + 1].append(mmr[j])
                per[5].append(ln)
                for j in range(8):
                    per[5 + j // 3].append(st[j])
                for j in range(8):
                    hooks[base + j] = per[j]
                    hooks_post[base + j] = post[j]
            else:
                hooks[base] = [lambda: WLOAD(wu, w_in_d[0], [("xst", 1), "wu"])]
        kv_dest = lambda b, h, tag: ((KT if tag == "k" else VT)[:, h * 4096 + b * 512: h * 4096 + b * 512 + 512],
                                     ("KT" if tag == "k" else "VT", h, b))
        run_units(units, lambda tag: (wbuf[1], ("wbuf", 1)) if tag == "k" else (wbuf[0], ("wbuf", 0)),
                  kv_dest, lambda tag: C_K, lambda tag: "qk" if tag == "k" else "v", hooks, hooks_post=hooks_post)
        for ch in range(4):
            for kc in range(8):
                MM(psb[4][:, ch * 16:(ch + 1) * 16], wu[:, kc * 512 + ch * 128: kc * 512 + ch * 128 + 128],
                   xn_blk(3, kc)[:, 496:512], kc == 0, kc == 7, ["wu", ("xn", 3, kc)], [("ps", 4)])
        if DBG["stop"] == 3:
            P.barrier()
        else:
            P.sync_on(["pe", "act", "dve", "pool", "sp"], ["dve", "pool", "sp"])
            P.sync_on(["pe"], ["act"])
        stop_here(3, [(KT, 8192), (VT, 4096)])

        NU = 16 + TOK
        WLOAD(wbuf[0], w_in_d[3], [("wbuf", 0)])
        WLOAD(wbuf[1], w_in_d[4], [("wbuf", 1)])
        u_off = [R_X, R_X + 8256, R_X + 2 * 8256, R_E]
        u_g = [view(o, NU, F32) for o in u_off]
        d_g = [view(o, TOK, BF16) for o in u_off]
        tmpA = view(R_G, NU, F32)
        tmpB = view(R_G + 8256, NU, F32)
        t16 = view(R_G + 2 * 8256, 16, F32)
        ev = {"n": 0}
        ubank = {"n": 0}

        def u_proj(ch):
            for bi in range(4):
                b = 4 + bi
                bank = ubank["n"] % 4
                ubank["n"] += 1
                for kc in range(8):
                    MM(psb[bank][:, :], wu[:, kc * 512 + ch * 128: kc * 512 + ch * 128 + 128], xn_blk(b, kc),
                       kc == 0, kc == 7, ["wu", ("xn", b, kc)], [("ps", bank)])
                CP("act", u_g[ch][:, 16 + bi * 512: 16 + bi * 512 + 512],
                   psb[bank][:, :], [("ps", bank)], [("u", ch)] + ([("xst", 0)] if ch == 3 else []))
                ev["n"] += 1
            CP("act", u_g[ch][:, 0:16], psb[4][:, ch * 16:(ch + 1) * 16], [("ps", 4)], [("u", ch)])

        def pooling(g):
            ug = u_g[g]
            cur, ckey = ug, ("u", g)
            vfrom = 0
            bufs = [tmpA, tmpB]
            for lvl in range(g + 1):
                s_ = 1 << lvl
                nxt = bufs[lvl % 2]
                nkey = ("ptmp", lvl % 2)
                v2 = vfrom + s_
                TT("dve", nxt[:, v2:NU], cur[:, v2:NU], cur[:, v2 - s_:NU - s_], ALU.add, [ckey], [nkey])
                cur, ckey, vfrom = nxt, nkey, v2
            w = float(1 << (g + 1))
            TT("dve", t16, cur[:, 16:32], cnt[:, g * 16:(g + 1) * 16], ALU.mult, [ckey, "cnt"], ["t16"])
            TT("dve", t16, t16, ug[:, 16:32], ALU.subtract, ["t16", ("u", g)], ["t16"])
            STT(d_g[g], cur[:, 16:NU], 1.0 / w, ug[:, 16:NU], ALU.mult, ALU.subtract,
                [ckey, ("u", g), "t16"], [("u", g), ("d", g)])
            CP("dve", d_g[g][:, 0:16], t16, ["t16", ("d", g)], [("d", g), ("u", g)])

        def pool_mm(g):
            for tb in range(4):
                bank = ubank["n"] % 4
                ubank["n"] += 1
                MM(psb[bank][:, :], poolw[:, g * 128:(g + 1) * 128], d_g[g][:, tb * 512: tb * 512 + 512],
                   True, True, ["poolw", ("d", g)], [("ps", bank)])
                ACT(pooled[:, g * TOK + tb * 512: g * TOK + tb * 512 + 512], psb[bank][:, :], AF.Copy,
                    [("ps", bank), "cols"], [("pooled", g, tb)] + (["wu"] if g < 2 else []), scale=col(C_PS + g))

        u_proj(3); pooling(3)
        u_proj(2); pooling(2)
        u_proj(1); pooling(1)
        pool_mm(3)
        u_proj(0); pooling(0)
        pool_mm(2); pool_mm(1); pool_mm(0)
        if DBG["stop"] == 2:
            P.barrier()
        else:
            P.sync_on(["pe", "act", "dve", "pool", "sp"], ["act", "dve", "pool", "sp"])
        stop_here(2, [(pooled, 8192)])

        cs_state["nbuf"] = 2
        cs_state["n"] = 1
        qunits_ = [(b, h, ("q", g)) for g in range(3) for b in range(4, 8) for h in range(4)]
        qhooks = {16: [lambda: WLOAD(wbuf[0], w_in_d[5], [("wbuf", 0)])]}
        run_units(qunits_, lambda tag: (wbuf[tag[1] % 2], ("wbuf", tag[1] % 2)),
                  lambda b, h, tag: (qt_head(tag[1] * 4 + h)[:, (b - 4) * 512:(b - 4) * 512 + 512], ("QT", tag[1] * 4 + h, b - 4)),
                  lambda tag: C_QS, lambda tag: "qk", qhooks, two_rq=True)
        if DBG["stop"] == 4:
            P.barrier()
        else:
            P.sync_on(["pe", "act", "dve", "pool", "sp"], ["dve", "pool", "sp"])
            P.sync_on(["pe"], ["act"])
        stop_here(4, [(QT_a, 8192)])

        numacc = view(R_Y, TOK, F32)
        denacc = view(R_Y + 8 * KB, TOK, F32)
        vblk = [view(R_Y + 16 * KB, 32 * 128, BF16), view(R_Y + 24 * KB, 32 * 128, BF16)]
        NPT = 12
        PtAll = view(200 * KB, NPT * 256, BF16)
        Pt = [PtAll[:, i * 256:(i + 1) * 256] for i in range(NPT)]
        wout = view(R_W, 8192, BF16)
        WLOAD(wout[:, 0:4096], w_out_d[:, 0:4096], [("wout", 0)])
        WLOAD(wout[:, 4096:8192], w_out_d[:, 4096:8192], [("wout", 1)])
        ev_ctr = {"n": 0}
        LAG = 5
        groups = [(h, g) for h in range(4) for g in range(3)]

        def grp_info(gi):
            h, g = groups[gi]
            dil = DILS[g]
            nq = 16 // dil
            return h, g, dil, nq

        def emit_vblocks(gi, only_batch=None):
            h, g, dil, nq = grp_info(gi)
            VTh = VT[:, h * 4096:(h + 1) * 4096]
            VT_keys = [("VT", h, b_) for b_ in range(8)]
            vbi = gi % 2
            VB = vblk[vbi]
            tiles = [(r, kt) for r in range(dil) for kt in range(nq + 1)]
            for t0 in range(0, len(tiles), 8):
                if only_batch is not None and t0 // 8 != only_batch:
                    continue
                grp = tiles[t0:t0 + 8]
                bank = 7
                pb = psb[bank][:, :].bitcast(BF16)
                for j, (r, kt) in enumerate(grp):
                    c0 = 2048 + r + dil * 128 * (kt - 1)
                    TR(pb[:, j * 128:(j + 1) * 128], VTh[:, c0: c0 + dil * 127 + 1: dil], ident, VT_keys + ["cb"], [("ps", bank)])
                s0 = grp[0][0] * (nq + 1) + grp[0][1]
                ncol = len(grp) * 128
                CP("act" if ev_ctr["n"] % 2 == 0 else "dve", VB[:, s0 * 128: s0 * 128 + ncol], pb[:, 0:ncol],
                   [("ps", bank)], [("vb", vbi, s0 + j) for j in range(len(grp))])
                ev_ctr["n"] += 1

        gunits = []
        for gi in range(len(groups)):
            h, g, dil, nq = grp_info(gi)
            lst = [(r, j) for r in range(dil) for j in range(nq)]
            for k_, (r, j) in enumerate(lst):
                gunits.append((gi, r, j, k_, len(lst)))

        def S_stage(i):
            gi, r, j, k_, ng = gunits[i]
            h, g, dil, nq = grp_info(gi)
            if k_ >= LAG + 1 and (k_ - LAG - 1) % 2 == 0 and gi + 1 < len(groups):
                emit_vblocks(gi + 1, only_batch=(k_ - LAG - 1) // 2)
            KTh = KT[:, h * 4096:(h + 1) * 4096]
            KT_keys = [("KT", h, b_) for b_ in range(8)]
            qh = g * 4 + h
            Qh = qt_head(qh)
            Q_keys = [("QT", qh, b_) for b_ in range(4)]
            slot = i % 3
            sp_ = psb[slot][:, 0:256]
            q0 = r + dil * 128 * j
            qap = Qh[:, q0: q0 + dil * 127 + 1: dil]
            for half in range(2):
                kt = j + half
                c0 = 2048 + r + dil * 128 * (kt - 1)
                MM(sp_[:, half * 128:(half + 1) * 128], KTh[:, c0: c0 + dil * 127 + 1: dil], qap, True, True,
                   KT_keys + Q_keys, [("ps", slot)])
            pslot = i % NPT
            pt = Pt[pslot]
            ACT(pt, sp_, AF.Exp, [("ps", slot)], [("pt", pslot)] + ([("cs", 1)] if i < NPT else []))
            TT("pool" if i % 3 == 2 else "dve", pt, pt, maskF if j == 0 else maskS, ALU.mult, [("pt", pslot), "cb"], [("pt", pslot)])

        def PV_stage(i):
            gi, r, j, k_, ng = gunits[i]
            h, g, dil, nq = grp_info(gi)
            vbi = gi % 2
            VB = vblk[vbi]
            pslot = i % NPT
            pt = Pt[pslot]
            bsel = (k_ // 4) % 2
            nb_, db_ = 3 + bsel, 5 + bsel
            cs = (k_ % 4) * 128
            for half in range(2):
                sl = r * (nq + 1) + j + half
                MM(psb[nb_][:, cs:cs + 128], VB[:, sl * 128:(sl + 1) * 128], pt[:, half * 128:(half + 1) * 128],
                   half == 0, half == 1, [("vb", vbi, sl), ("pt", pslot)], [("ps", nb_)])
            if k_ % 4 == 3:
                s0_ = pslot - 3
                for half in range(2):
                    MM(psb[db_][:, 0:512], ones, ap2(PtAll, 0, 128, s0_ * 256 + half * 128, [[256, 4], [1, 128]]),
                       half == 0, half == 1, ["ones"] + [("pt", s0_ + q_) for q_ in range(4)], [("ps", db_)])
                r0, j0 = gunits[i - 3][1], gunits[i - 3][2]
                if dil == 1:
                    dims = [[1, 512]]
                    c0 = 128 * j0
                elif dil == 4:
                    dims = [[4, 512]]
                    c0 = r0
                else:
                    dims = [[1, 4], [16, 128]]
                    c0 = r0
                for (acc, bank_, key) in ((numacc, nb_, "nacc"), (denacc, db_, "dacc")):
                    oap = ap2(acc, 0, 128, c0, dims)
                    if len(dims) == 2:
                        iap = ap2(psb[bank_][:, :], 0, 128, 0, [[128, 4], [1, 128]])
                    else:
                        iap = psb[bank_][:, :]
                    if g == 0:
                        CP("dve", oap, iap, [("ps", bank_)], [(key, j0 // 4)])
                    else:
                        keys4 = [(key, t_) for t_ in range(4)]
                        TT("dve", oap, iap, oap, ALU.add, [("ps", bank_)] + keys4, keys4)
            if g == 2 and k_ == ng - 1:
                for tb in range(4):
                    def fin(tb=tb, h=h):
                        dv = denacc[:, tb * 512:(tb + 1) * 512]
                        ACT(dv, dv, AF.Ln, [("dacc", tb)], [("dacc", tb)])
                        ACT(dv, dv, AF.Exp, [("dacc", tb)], [("dacc", tb)], scale=-1.0)
                        TT("dve", attnT[:, h * TOK + tb * 512: h * TOK + tb * 512 + 512],
                           numacc[:, tb * 512:(tb + 1) * 512], dv, ALU.mult,
                           [("nacc", tb), ("dacc", tb)], [("attnT", h, tb)])
                    deferred.setdefault(i + LAG + 1 + tb, []).append(fin)

        emit_vblocks(0)
        deferred = {}
        for i in range(len(gunits) + LAG):
            if i < len(gunits):
                S_stage(i)
            if 0 <= i - LAG < len(gunits):
                PV_stage(i - LAG)
            for f in deferred.pop(i, ()):
                f()
        for k_ in sorted(deferred):
            for f in deferred[k_]:
                f()
        hT = view(R_X, 8 * TOK, F32)
        hT3 = hT.rearrange("p (k t) -> p k t", k=8)
        memf = view(R_E, 8 * 256, F32)
        early_x = DBG["stop"] != 5
        if early_x:
            DMA("sp", hT3[:, 0:4, 0:512], xT_v[:, 0:4, HALO:HALO + 512], (),
                [("hT", oc, 0) for oc in range(4)] + [("QT", qh, b_) for qh in range(8) for b_ in range(4)])
            DMA("sp", memf.rearrange("p (k t) -> p k t", k=8), memT.rearrange("(k p) t -> p k t", p=128), (),
                ["memf"] + [("QT", qh, b_) for qh in range(8, 12) for b_ in range(4)])
        if DBG["stop"] == 5:
            P.barrier()
        else:
            P.sync_on(["pe", "act", "dve", "pool", "sp"], ["act", "dve", "pool", "sp"])
        stop_here(5, [(attnT, 8192)])

        if not early_x:
            DMA("sp", memf.rearrange("p (k t) -> p k t", k=8), memT.rearrange("(k p) t -> p k t", p=128), (), ["memf"])
        for tb in range(4):
            for hf in range(2):
                if early_x and tb == 0 and hf == 0:
                    continue
                DMA("sp", hT3[:, 4 * hf:4 * hf + 4, tb * 512:(tb + 1) * 512],
                    xT_v[:, 4 * hf:4 * hf + 4, HALO + tb * 512: HALO + tb * 512 + 512], (),
                    [("hT", oc, tb) for oc in range(4 * hf, 4 * hf + 4)])

        def mix_src(kc, tb):
            if kc < 4:
                return attnT[:, kc * TOK + tb * 512: kc * TOK + tb * 512 + 512], ("attnT", kc, tb)
            return pooled[:, (kc - 4) * TOK + tb * 512:(kc - 4) * TOK + tb * 512 + 512], ("pooled", kc - 4, tb)

        hn = view(R_C, 8 * TOK, BF16)
        wckv = view(R_D, 8192, BF16)
        wcq = view(R_D + 16 * KB, 4096, BF16)
        wco = view(R_D + 24 * KB, 4096, BF16)
        WLOAD(wckv[:, 0:4096], w_ckv_d[:, 0:4096], [("wckv", 0)])
        WLOAD(wckv[:, 4096:8192], w_ckv_d[:, 4096:8192], [("wckv", 1)])
        WLOAD(wcq, w_cq_d, ["wcq"])
        memf = view(R_E, 8 * 256, F32)
        memn = view(R_E + 8 * KB, 8 * 256, BF16)

        def hn_steps(tb):
            return rms_steps(lambda kc, tb=tb: hT[:, kc * TOK + tb * 512: kc * TOK + tb * 512 + 512],
                             lambda kc, tb=tb: hn[:, kc * TOK + tb * 512: kc * TOK + tb * 512 + 512], C_CROSS,
                             [("hT", oc, tb) for oc in range(8)], lambda kc, tb=tb: ("hn", kc, tb))

        RORDER = [0, 1, 2, 3, 8, 9, 10, 11, 4, 5, 6, 7, 12, 13, 14, 15] + list(range(16, 25))
        def mem_rms_steps():
            th = []
            for kc in range(8):
                def f(kc=kc):
                    ACT(rsq[kc % 4][:, 0:256], memf[:, kc * 256:(kc + 1) * 256], AF.Square, ["memf"], [("rsq", kc % 4)])
                    MM(psb[7][:, 0:256], ones, rsq[kc % 4][:, 0:256], kc == 0, kc == 7, [("rsq", kc % 4), "ones"], [("ps", 7)])
                th.append(f)

            def g_():
                ACT(rsd[:, 0:256], psb[7][:, 0:256], AF.Ln, [("ps", 7), "cols_eps"], ["rsd"], bias=col(C_EPS), scale=1.0 / D)
                ACT(rsd[:, 0:256], rsd[:, 0:256], AF.Exp, ["rsd"], ["rsd"], scale=-0.5)
            th.append(g_)
            for kc in range(8):
                def h_(kc=kc):
                    STT(memn[:, kc * 256:(kc + 1) * 256], memf[:, kc * 256:(kc + 1) * 256], col(C_MEM + kc), rsd[:, 0:256],
                        ALU.mult, ALU.mult, ["memf", "rsd", "cols"], ["memn"])
                th.append(h_)
            return th

        n_ = 0
        hn3_pend = []
        for tb in range(5):
            pend = []
            if tb >= 1:
                st_ = hn_steps(tb - 1)
                pend = [st_[k_] for k_ in RORDER]
            if tb == 1:
                pend = pend + mem_rms_steps()
            if tb == 4:
                hn3_pend = pend
                break
            for oc in range(8):
                bank = n_ % 4
                n_ += 1
                for kc in range(8):
                    src, skey = mix_src(kc, tb)
                    MM(psb[bank][:, :], wout[:, kc * 1024 + oc * 128: kc * 1024 + oc * 128 + 128], src, kc == 0, kc == 7,
                       [("wout", kc // 4), skey], [("ps", bank)])
                hv = hT[:, oc * TOK + tb * 512: oc * TOK + tb * 512 + 512]
                TT("dve", hv, psb[bank][:, :], hv, ALU.add, [("ps", bank), ("hT", oc, tb)], [("hT", oc, tb)])
                lo, hi_ = (len(pend) * oc) // 8, (len(pend) * (oc + 1)) // 8
                for f in pend[lo:hi_]:
                    f()
        if DBG["stop"] == 6:
            P.barrier()
        else:
            P.sync_on(["pe"], ["act", "dve", "pool", "sp"])
        stop_here(6, [(hT, 8192)])
        WLOAD(wco, w_co_d, ["wco"])

        qcT = view(R_F, 4 * TOK, BF16)
        ocT = view(R_W, 4 * TOK, BF16)
        kcT = view(R_E + 12 * KB, 4 * 256, BF16)
        vc = view(R_E + 14 * KB, 2 * 512, BF16)
        sqb = [view(R_G + i * KB, 512, BF16) for i in range(2)]
        sdr = [view(R_G + 2 * KB + i * 2 * KB, 512, F32) for i in range(2)]
        Pc = [view(R_G + 6 * KB + i * KB, 512, BF16) for i in range(6)]
        rdn = [view(R_G + 12 * KB + i * 2 * KB, 512, F32) for i in range(2)]
        for hh in range(4):
            bank = hh % 2
            for kc in range(8):
                MM(psb[bank][:, 0:256], wckv[:, kc * 1024 + hh * 128: kc * 1024 + hh * 128 + 128], memn[:, kc * 256:(kc + 1) * 256],
                   kc == 0, kc == 7, [("wckv", kc // 4), "memn"], [("ps", bank)])
            s = sqb[hh % 2]
            ACT(s[:, 0:256], psb[bank][:, 0:256], AF.Square, [("ps", bank)], [("sq", hh % 2)])
            sb = 2 + hh % 2
            MM(psb[sb][:, 0:256], ones, s[:, 0:256], True, True, [("sq", hh % 2), "ones"], [("ps", sb)])
            sd = sdr[hh % 2]
            ACT(sd[:, 0:256], psb[sb][:, 0:256], AF.Ln, [("ps", sb), "cols_eps"], [("sdr", hh % 2)], bias=col(C_EPS), scale=1.0 / 128)
            ACT(sd[:, 0:256], sd[:, 0:256], AF.Exp, [("sdr", hh % 2)], [("sdr", hh % 2)], scale=-0.5)
            STT(kcT[:, hh * 256:(hh + 1) * 256], psb[bank][:, 0:256], col(C_CK), sd[:, 0:256], ALU.mult, ALU.mult,
                [("ps", bank), ("sdr", hh % 2), "cols"], [("kcT", hh)])
            for _ in range(5):
                if hn3_pend:
                    hn3_pend.pop(0)()
        for mt in range(2):
            bank = 4 + mt
            for kc in range(8):
                MM(psb[bank][:, :], memn[:, kc * 256 + mt * 128: kc * 256 + mt * 128 + 128], wckv[:, kc * 1024 + 512: kc * 1024 + 1024],
                   kc == 0, kc == 7, [("wckv", kc // 4), "memn"], [("ps", bank)])
            CP("act", vc[:, mt * 512:(mt + 1) * 512], psb[bank][:, :], [("ps", bank)], [("vc", mt)])
            for _ in range(5):
                if hn3_pend:
                    hn3_pend.pop(0)()
        while hn3_pend:
            hn3_pend.pop(0)()
        qunits = [(tb, hh) for tb in range(4) for hh in range(4)]

        def q_st0(i):
            tb, hh = qunits[i]
            bank = i % 3
            for kc in range(8):
                MM(psb[bank][:, :], wcq[:, kc * 512 + hh * 128: kc * 512 + hh * 128 + 128],
                   hn[:, kc * TOK + tb * 512: kc * TOK + tb * 512 + 512], kc == 0, kc == 7,
                   ["wcq", ("hn", kc, tb)], [("ps", bank)])

        def q_st1(i):
            tb, hh = qunits[i]
            bank = i % 3
            s_ = sqb[i % 2]
            ACT(s_, psb[bank][:, :], AF.Square, [("ps", bank)], [("sq", i % 2)])
            sb = 3 + i % 2
            MM(psb[sb][:, :], ones, s_, True, True, [("sq", i % 2), "ones"], [("ps", sb)])
            sd = sdr[i % 2]
            ACT(sd, psb[sb][:, :], AF.Ln, [("ps", sb), "cols_eps"], [("sdr", i % 2)], bias=col(C_EPS), scale=1.0 / 128)
            ACT(sd, sd, AF.Exp, [("sdr", i % 2)], [("sdr", i % 2)], scale=-0.5)
            STT(qcT[:, hh * TOK + tb * 512: hh * TOK + tb * 512 + 512], psb[bank][:, :], col(C_CQS), sd, ALU.mult, ALU.mult,
                [("ps", bank), ("sdr", i % 2), "cols_qs"], [("qcT", hh, tb)])

        for i in range(len(qunits) + 1):
            if i < len(qunits):
                q_st0(i)
            if i >= 1:
                q_st1(i - 1)
        citems = [(tb, hh, mt) for tb in range(4) for hh in range(4) for mt in range(2)]

        def c_S(k):
            tb, hh, mt = citems[k]
            sbank = k % 4
            MM(psb[sbank][:, :], kcT[:, hh * 256 + mt * 128: hh * 256 + mt * 128 + 128],
               qcT[:, hh * TOK + tb * 512: hh * TOK + tb * 512 + 512], True, True,
               [("kcT", hh), ("qcT", hh, tb)], [("ps", sbank)])
            ACT(Pc[k % 6], psb[sbank][:, :], AF.Exp, [("ps", sbank)], [("pc", k % 6)])

        def c_PV(k):
            tb, hh, mt = citems[k]
            u_ = k // 2
            nbank = 4 + u_ % 2
            dbank = 6 + u_ % 2
            pc = Pc[k % 6]
            pk = ("pc", k % 6)
            MM(psb[nbank][:, :], vc[:, mt * 512 + hh * 128: mt * 512 + hh * 128 + 128], pc, mt == 0, mt == 1,
               [("vc", mt), pk], [("ps", nbank)])
            MM(psb[dbank][:, :], ones, pc, mt == 0, mt == 1, ["ones", pk], [("ps", dbank)])
            if mt == 1:
                rd = rdn[u_ % 2]
                ACT(rd, psb[dbank][:, :], AF.Ln, [("ps", dbank)], [("rdn", u_ % 2)])
                ACT(rd, rd, AF.Exp, [("rdn", u_ % 2)], [("rdn", u_ % 2)], scale=-1.0)
                TT("dve", ocT[:, hh * TOK + tb * 512: hh * TOK + tb * 512 + 512], psb[nbank][:, :], rd, ALU.mult,
                   [("ps", nbank), ("rdn", u_ % 2)], [("ocT", hh, tb)])

        CL = 3
        for k in range(len(citems) + CL):
            if k < len(citems):
                c_S(k)
            if k - CL >= 0:
                c_PV(k - CL)
        hnh = view(R_C, 8 * 1024, BF16)
        actT = view(R_C + 16 * KB, 22 * 1024, BF16)
        wgu = [view(R_E + i * 8 * KB, 4096, BF16) for i in range(2)]
        wdn = [view(152 * KB + i * 11264, 22 * 256, BF16) for i in range(2)]
        sgb = [view(132 * KB + i * 2 * KB, 512, F32) for i in range(2)]
        ost = [view(176 * KB + i * 2 * KB, 512, F32) for i in range(3)]
        sqb = [view(R_G + i * KB, 512, BF16) for i in range(2)]
        sdr = [view(R_G + 2 * KB + i * 2 * KB, 512, F32) for i in range(2)]
        outT_v = outT.rearrange("(k p) t -> p k t", p=128)
        final_ops = []
        tasks = []
        for half in range(2):
            for fg in range(11):
                tasks.append(("gu", half, fg))
            for og in range(4):
                tasks.append(("dn", half, og))
        cnts = {"gu": 0, "dn": 0}
        tbuf = []
        for (kind, half, idx) in tasks:
            tbuf.append(cnts[kind] % 2)
            cnts[kind] += 1

        def issue_load(k):
            kind, half, idx = tasks[k]
            wi = tbuf[k]
            if kind == "gu":
                WLOAD(wgu[wi], w_gu_d[idx], [("wgu", wi), "memf", "memn"] + [("kcT", h_) for h_ in range(4)] + [("vc", m_) for m_ in range(2)])
            else:
                WLOAD(wdn[wi], w_d_d[idx], [("wdn", wi)])

        issued = set()

        def issue_once(k):
            if k not in issued and k < len(tasks):
                issued.add(k)
                issue_load(k)

        issue_once(0)
        issue_once(1)

        def ffn_rms_steps(tb, t2):
            st_ = rms_steps(lambda kc, tb=tb: hT[:, kc * TOK + tb * 512: kc * TOK + tb * 512 + 512],
                            lambda kc, t2=t2: hnh[:, kc * 1024 + t2 * 512: kc * 1024 + t2 * 512 + 512], C_FFN,
                            [("hT2", oc, tb) for oc in range(8)],
                            lambda kc, t2=t2: [("hnh", kc, t2)] + [("hn", kc_, tb_) for kc_ in range(8) for tb_ in range(4)],
                            bank=7)
            return [st_[k_] for k_ in RORDER]

        n_ = 0
        for tb in range(4):
            pend = ffn_rms_steps(tb - 1, tb - 1) if tb in (1, 2) else []
            for oc in range(8):
                bank = n_ % 4
                n_ += 1
                for kc in range(4):
                    MM(psb[bank][:, :], wco[:, kc * 1024 + oc * 128: kc * 1024 + oc * 128 + 128],
                       ocT[:, kc * TOK + tb * 512: kc * TOK + tb * 512 + 512], kc == 0, kc == 3,
                       ["wco", ("ocT", kc, tb)], [("ps", bank)])
                hv = hT[:, oc * TOK + tb * 512: oc * TOK + tb * 512 + 512]
                TT("dve", hv, psb[bank][:, :], hv, ALU.add, [("ps", bank), ("hT2", oc, tb)], [("hT2", oc, tb)])
                lo, hi_ = (len(pend) * oc) // 8, (len(pend) * (oc + 1)) // 8
                for f in pend[lo:hi_]:
                    f()
        if DBG["stop"] == 7:
            P.barrier()
        else:
            P.sync_on(["pe", "act", "dve", "pool", "sp"], ["act", "dve", "pool", "sp"])
        stop_here(7, [(hT, 8192)])

        st = {"n": 0, "on": 0}
        ffn_pend = []

        def do_gu(half, fg, wi):
            for fl in range(2):
                f = fg * 2 + fl
                for t2 in range(2):
                    n_ = st["n"]
                    gb = (2 * n_) % 4
                    ub = (2 * n_ + 1) % 4
                    for kc in range(8):
                        MM(psb[gb][:, :], wgu[wi][:, kc * 512 + fl * 128: kc * 512 + fl * 128 + 128],
                           hnh[:, kc * 1024 + t2 * 512: kc * 1024 + t2 * 512 + 512], kc == 0, kc == 7,
                           [("wgu", wi), ("hnh", kc, t2)], [("ps", gb)])
                    for kc in range(8):
                        MM(psb[ub][:, :], wgu[wi][:, kc * 512 + 256 + fl * 128: kc * 512 + 256 + fl * 128 + 128],
                           hnh[:, kc * 1024 + t2 * 512: kc * 1024 + t2 * 512 + 512], kc == 0, kc == 7,
                           [("wgu", wi), ("hnh", kc, t2)], [("ps", ub)])
                    sg = sgb[n_ % 2]
                    ACT(sg, psb[gb][:, :], AF.Silu, [("ps", gb)], [("sg", n_ % 2)])
                    TT("dve", actT[:, f * 1024 + t2 * 512: f * 1024 + t2 * 512 + 512], sg, psb[ub][:, :], ALU.mult,
                       [("sg", n_ % 2), ("ps", ub)], [("actT", f, t2)])
                    st["n"] += 1

        def do_dn(half, og, wi):
            for ol in range(2):
                oc = og * 2 + ol
                for t2 in range(2):
                    tb = half * 2 + t2
                    on = st["on"]
                    bank = 4 + on % 2
                    for f in range(22):
                        MM(psb[bank][:, :], wdn[wi][:, f * 256 + ol * 128: f * 256 + ol * 128 + 128],
                           actT[:, f * 1024 + t2 * 512: f * 1024 + t2 * 512 + 512], f == 0, f == 21,
                           [("wdn", wi), ("actT", f, t2)], [("ps", bank)])
                    o_ = ost[on % 3]
                    TT("dve", o_, psb[bank][:, :], hT[:, oc * TOK + tb * 512: oc * TOK + tb * 512 + 512], ALU.add,
                       [("ps", bank)], [("ost", on % 3)])
                    final_ops.append(DMA("sp", outT_v[:, oc, tb * 512:(tb + 1) * 512], o_, [("ost", on % 3)], ()))
                    st["on"] += 1
                    for _ in range(5):
                        if ffn_pend:
                            ffn_pend.pop(0)()

        for k, (kind, half, idx) in enumerate(tasks):
            issue_once(k)
            issue_once(k + 1)
            if kind == "gu":
                do_gu(half, idx, tbuf[k])
            else:
                if half == 0 and idx == 1:
                    for t2 in range(2):
                        ffn_pend.extend(ffn_rms_steps(2 + t2, t2))
                do_dn(half, idx, tbuf[k])
                if half == 0 and idx == 3:
                    while ffn_pend:
                        ffn_pend.pop(0)()
        P.finalize_and_emit(final_ops)
        DBG["stats"] = P.stats
    return nc


def _tile_w(w, ncols_group):
    K, N = w.shape
    kc = K // 128
    ng = N // ncols_group
    t = w.reshape(kc, 128, ng, ncols_group).transpose(2, 1, 0, 3).reshape(ng, 128, kc * ncols_group)
    return np.ascontiguousarray(t)


_NC_CACHE = {}


def kernel(x, mem, positions, mix_norm_g, w_in, q_norm_g, k_norm_g, pool_w, pool_scale, w_out,
           cross_norm_g, mem_norm_g, w_cq, w_ckv, cq_norm_g, ck_norm_g, w_co, ffn_norm_g, w_gate_up, w_down):
    f32 = np.float32
    x = np.asarray(x, f32)
    mem = np.asarray(mem, f32)
    positions = np.asarray(positions, np.int32)
    B, S, _ = x.shape
    cpb = S // TOK

    w_in0 = np.asarray(w_in, f32)[0]
    order = [(2560, 3072), (1536, 2048), (2048, 2560), (0, 512), (512, 1024), (1024, 1536)]
    w_in_t = np.stack([_tile_w(w_in0[:, a:b], 512)[0] for a, b in order])
    w_out_t = _tile_w(np.asarray(w_out, f32)[0], 1024)[0]
    w_cq_t = _tile_w(np.asarray(w_cq, f32)[0], 512)[0]
    w_ckv_t = _tile_w(np.asarray(w_ckv, f32)[0], 1024)[0]
    w_co_t = _tile_w(np.asarray(w_co, f32)[0], 1024)[0]
    wgu = np.asarray(w_gate_up, f32)[0]
    gate_t = _tile_w(wgu[:, :DFF], 256)
    up_t = _tile_w(wgu[:, DFF:], 256)
    w_gu_t = np.ascontiguousarray(
        np.concatenate([gate_t.reshape(11, 128, 8, 256), up_t.reshape(11, 128, 8, 256)], axis=3).reshape(11, 128, 4096))
    w_d_t = _tile_w(np.asarray(w_down, f32)[0], 256)
    pool_w_t = np.ascontiguousarray(np.asarray(pool_w, f32)[0].transpose(1, 0, 2).reshape(128, 512))

    cols = np.zeros((128, NCOL), f32)

    def colmat(v):
        v = np.asarray(v, f32).reshape(-1)
        return v.reshape(-1, 128).T

    cols[:, C_MIX:C_MIX + 8] = colmat(mix_norm_g)
    cols[:, C_CROSS:C_CROSS + 8] = colmat(cross_norm_g)
    cols[:, C_MEM:C_MEM + 8] = colmat(mem_norm_g)
    cols[:, C_FFN:C_FFN + 8] = colmat(ffn_norm_g)
    cols[:, C_Q] = np.asarray(q_norm_g, f32).reshape(-1)
    cols[:, C_K] = np.asarray(k_norm_g, f32).reshape(-1)
    cols[:, C_CQ] = np.asarray(cq_norm_g, f32).reshape(-1)
    cols[:, C_CK] = np.asarray(ck_norm_g, f32).reshape(-1)
    cols[:, C_PS:C_PS + 4] = colmat(pool_scale)
    inv_freq = (500000.0 ** (-np.arange(0, 32, 2, dtype=np.float32) / np.float32(32))).astype(f32)
    cols[:, C_INVF] = np.tile(inv_freq, 8)

    kk = np.arange(128)[:, None]
    qq = np.arange(128)[None, :]
    cbs = np.zeros((128, NCB), f32)
    cbs[:, CB_ID:CB_ID + 128] = np.eye(128, dtype=f32)
    rm = np.zeros((128, 128), f32)
    for m in range(16):
        rm[m + 16, m] = -1.0
        rm[m, m + 16] = 1.0
    cbs[:, CB_RM:CB_RM + 128] = rm
    prev = (kk >= qq).astype(f32)
    cur = (kk <= qq).astype(f32)
    cbs[:, CB_MS:CB_MS + 256] = np.concatenate([prev, cur], axis=1)
    cb_first = cbs.copy()
    cb_other = cbs.copy()
    cb_first[:, CB_MF:CB_MF + 256] = np.concatenate([np.zeros_like(prev), cur], axis=1)
    cb_other[:, CB_MF:CB_MF + 256] = np.concatenate([prev, cur], axis=1)
    cnt_first = np.zeros((128, 64), f32)
    cnt_other = np.zeros((128, 64), f32)
    for g, w in enumerate((2, 4, 8, 16)):
        t = np.arange(16)
        cnt_first[:, g * 16:(g + 1) * 16] = (1.0 / np.minimum(t + 1, w)).astype(f32)[None, :]
        cnt_other[:, g * 16:(g + 1) * 16] = np.float32(1.0 / w)

    in_maps = []
    for c in range(N_CORES):
        b = c // cpb
        s0 = (c % cpb) * TOK
        xt = np.zeros((D, HALO + TOK), f32)
        pp = np.zeros((1, HALO + TOK), np.int32)
        if s0 > 0:
            xt[:, :HALO] = x[b, s0 - HALO:s0, :].T
            pp[0, :HALO] = positions[b, s0 - HALO:s0]
        xt[:, HALO:] = x[b, s0:s0 + TOK, :].T
        pp[0, HALO:] = positions[b, s0:s0 + TOK]
        first = (s0 == 0)
        in_maps.append({
            "xT": xt, "posb": np.ascontiguousarray(np.repeat(pp.reshape(8, 1, 512), 16, axis=1).reshape(128, 512)), "memT": np.ascontiguousarray(mem[b].T), "cols": cols,
            "cb": cb_first if first else cb_other, "cntinv": cnt_first if first else cnt_other,
            "w_in_t": w_in_t, "w_out_t": w_out_t, "w_cq_t": w_cq_t, "w_ckv_t": w_ckv_t, "w_co_t": w_co_t,
            "w_gu_t": w_gu_t, "w_d_t": w_d_t, "pool_w_t": pool_w_t,
        })

    if "nc" not in _NC_CACHE:
        _NC_CACHE["nc"] = build_program()
    nc = _NC_CACHE["nc"]
    res = run_bass_kernel_spmd(nc, in_maps, core_ids=list(range(N_CORES)))
    DBG["res"] = res
    out = np.empty((B, S, D), f32)
    for c in range(N_CORES):
        b = c // cpb
        s0 = (c % cpb) * TOK
        out[b, s0:s0 + TOK, :] = res.results[c]["outT"].T
    return out
```
